# Optimizing a Trainium2 kernel written in Bass

```python
import math
import jax
import jax.numpy as jnp
from jax import lax
import numpy as np

D_MODEL = 2048
BATCH = 4
SEQ = 2048
DEPTH = 2
DEC_BATCH = 2
DEC_SEQ = 4096
PAST_LEN = 128

GRID_W = 64
NA_HEADS = 8
NA_HEAD_DIM = 64
NA_ROWS = 8
NA_COLS = 16
DIFF_HEADS = 4
DIFF_QK_DIM = 64
DIFF_V_DIM = 128
DIFF_QBLOCK = 128
DIL_WINDOWS = (128, 512, 2048)
DIL_DILATIONS = (1, 4, 16)
N_DIL_GROUPS = 3
DIL_HEADS = 4
DIL_HEAD_DIM = 128
SWA_Q_HEADS = 8
SWA_KV_HEADS = 2
SWA_GROUP = SWA_Q_HEADS // SWA_KV_HEADS
SWA_HEAD_DIM = 64
SWA_RADIUS = 128
N_BRANCHES = 4
BRANCH_WIDTH = 512
D_FF = ((8 * D_MODEL // 3 + 127) // 128) * 128
N_ALIBI = SWA_Q_HEADS + N_DIL_GROUPS * DIL_HEADS + DIFF_HEADS
NA_W = NA_HEADS * NA_HEAD_DIM
DIFF_QK_W = DIFF_HEADS * 2 * DIFF_QK_DIM
DIFF_V_W = DIFF_HEADS * DIFF_V_DIM
DIL_W = N_DIL_GROUPS * DIL_HEADS * DIL_HEAD_DIM
SWA_Q_W = SWA_Q_HEADS * SWA_HEAD_DIM
SWA_KV_W = SWA_KV_HEADS * SWA_HEAD_DIM
IN_SPLITS = (NA_W, NA_W, NA_W, DIFF_QK_W, DIFF_QK_W, DIFF_V_W, DIL_W, DIL_W, DIL_W, SWA_Q_W, SWA_KV_W, SWA_KV_W)
IN_WIDTH = NA_W * 3 + DIFF_QK_W * 2 + DIFF_V_W + DIL_W * 3 + SWA_Q_W + SWA_KV_W * 2
RMS_EPS = 1e-6
NEG_INF = -1e30

kernel_name = 'hybrid_bidir_gated_parallel_encoder'


def rmsnorm(x, g, eps=RMS_EPS):
    x32 = x.astype(jnp.float32)
    y = x32 * lax.rsqrt(jnp.mean(x32 * x32, axis=-1, keepdims=True) + eps)
    return (y * g.astype(jnp.float32)).astype(x.dtype)


def swiglu(x, w_in, w_out):
    gate, up = jnp.split(x @ w_in, 2, axis=-1)
    return (jax.nn.silu(gate) * up) @ w_out


def alibi_slopes(n):
    return jnp.exp2(-8.0 * jnp.arange(1, n + 1, dtype=jnp.float32) / n)


def split_cols(a, sizes):
    parts, start = [], 0
    for size in sizes:
        parts.append(a[..., start:start + size])
        start += size
    return parts


def neighborhood_attention(q, k, v, rpb):
    b, t, h, dh = q.shape
    rows = t // GRID_W
    kh = min(NA_ROWS, rows)
    ncb = GRID_W // NA_COLS
    r = jnp.arange(rows)
    row_idx = jnp.clip(r - kh // 2, 0, rows - kh)[:, None] + jnp.arange(kh)[None, :]
    cb = jnp.arange(ncb)
    col_idx = jnp.clip(cb * NA_COLS - NA_COLS // 2, 0, GRID_W - 2 * NA_COLS)[:, None] + jnp.arange(2 * NA_COLS)[None, :]
    qcol = cb[:, None] * NA_COLS + jnp.arange(NA_COLS)[None, :]
    cstart = jnp.clip(qcol - NA_COLS // 2, 0, GRID_W - NA_COLS)
    ccol = col_idx[:, None, :]
    col_ok = (ccol >= cstart[..., None]) & (ccol < cstart[..., None] + NA_COLS)
    ri = row_idx[:, None, :, None]
    ci = col_idx[None, :, None, :]
    kg = k.reshape(b, rows, GRID_W, h, dh)[:, ri, ci]
    vg = v.reshape(b, rows, GRID_W, h, dh)[:, ri, ci]
    qg = q.reshape(b, rows, ncb, NA_COLS, h, dh)
    s = jnp.einsum('brnqhd,brnkwhd->bhrnqkw', qg, kg, preferred_element_type=jnp.float32) * dh ** -0.5
    dr = row_idx - r[:, None] + NA_ROWS - 1
    dc = jnp.clip(ccol - qcol[..., None] + NA_COLS - 1, 0, 2 * NA_COLS - 2)
    bias = rpb.astype(jnp.float32)[:, dr[:, None, None, :, None], dc[None, :, :, None, :]]
    s = jnp.where(col_ok[:, :, None, :], s + bias, NEG_INF)
    p = jax.nn.softmax(s.reshape(s.shape[:-2] + (-1,)), axis=-1).reshape(s.shape)
    out = jnp.einsum('bhrnqkw,brnkwhd->brnqhd', p.astype(v.dtype), vg)
    return out.reshape(b, t, h * dh)


def diff_attention(q, k, v, lam_vecs, subln_g, slopes, lam_init):
    b, t, h, _, dk = q.shape
    nqb = t // DIFF_QBLOCK
    lv = lam_vecs.astype(jnp.float32)
    lam = jnp.exp(jnp.sum(lv[0] * lv[1])) - jnp.exp(jnp.sum(lv[2] * lv[3])) + lam_init
    kpos = jnp.arange(t)
    q_blocks = q.reshape(b, nqb, DIFF_QBLOCK, h, 2, dk).transpose(1, 0, 2, 3, 4, 5)

    def one_block(args):
        q_blk, i = args
        s = jnp.einsum('bqhmd,bkhmd->bhmqk', q_blk, k, preferred_element_type=jnp.float32) * dk ** -0.5
        qpos = i * DIFF_QBLOCK + jnp.arange(DIFF_QBLOCK)
        dist = jnp.abs(qpos[:, None] - kpos[None, :]).astype(jnp.float32)
        p = jax.nn.softmax(s - slopes[None, :, None, None, None] * dist, axis=-1)
        a = p[:, :, 0] - lam * p[:, :, 1]
        return jnp.einsum('bhqk,bkhd->bqhd', a.astype(v.dtype), v)

    o = lax.map(one_block, (q_blocks, jnp.arange(nqb)))
    o = o.transpose(1, 0, 2, 3, 4).reshape(b, t, h, v.shape[-1])
    o = rmsnorm(o, subln_g, eps=1e-5) * (1.0 - lam_init)
    return o.reshape(b, t, h * v.shape[-1])


def banded_attention(q, k, v, radius, step, slopes, sink=None):
    n, length, hk, g, dh = q.shape
    blk = radius
    nb = -(-length // blk)
    lp = nb * blk
    q = jnp.pad(q, ((0, 0), (0, lp - length), (0, 0), (0, 0), (0, 0)))
    kv_pad = ((0, 0), (blk, lp - length + blk), (0, 0), (0, 0))
    k = jnp.pad(k, kv_pad)
    v = jnp.pad(v, kv_pad)
    kidx = jnp.arange(nb)[:, None] * blk + jnp.arange(3 * blk)[None, :]
    kb = k[:, kidx]
    vb = v[:, kidx]
    qb = q.reshape(n, nb, blk, hk, g, dh)
    qpos = jnp.arange(nb)[:, None] * blk + jnp.arange(blk)[None, :]
    kpos = kidx - blk
    dist = jnp.abs(qpos[:, :, None] - kpos[:, None, :])
    valid = (dist <= radius) & (kpos[:, None, :] >= 0) & (kpos[:, None, :] < length)
    s = jnp.einsum('bnqhgd,bnkhd->bhgnqk', qb, kb, preferred_element_type=jnp.float32) * dh ** -0.5
    s = s - (slopes.astype(jnp.float32)[:, :, None, None, None] * step) * dist.astype(jnp.float32)
    s = jnp.where(valid, s, NEG_INF)
    lse = jax.nn.logsumexp(s, axis=-1)
    if sink is not None:
        lse = jnp.logaddexp(lse, sink.astype(jnp.float32)[:, :, None, None])
    p = jnp.exp(s - lse[..., None])
    out = jnp.einsum('bhgnqk,bnkhd->bnqhgd', p.astype(v.dtype), vb)
    out = out.reshape(n, lp, hk, g, dh)[:, :length]
    lse = lse.transpose(0, 3, 4, 1, 2).reshape(n, lp, hk, g)[:, :length]
    return out, lse


def dilated_attention(q, k, v, slopes):
    b, t = q.shape[:2]
    outs, lses = [], []
    for gi in range(N_DIL_GROUPS):
        dil = DIL_DILATIONS[gi]
        radius = DIL_WINDOWS[gi] // (2 * dil)
        n_sub = t // dil

        def to_sub(a, gi=gi, dil=dil, n_sub=n_sub):
            a = a[:, :, gi].reshape(b, n_sub, dil, DIL_HEADS, DIL_HEAD_DIM)
            return a.transpose(0, 2, 1, 3, 4).reshape(b * dil, n_sub, DIL_HEADS, DIL_HEAD_DIM)

        o, lse = banded_attention(to_sub(q)[:, :, :, None], to_sub(k), to_sub(v), radius, dil, slopes[gi][:, None])
        o = o[:, :, :, 0].reshape(b, dil, n_sub, DIL_HEADS, DIL_HEAD_DIM).transpose(0, 2, 1, 3, 4)
        lse = lse[:, :, :, 0].reshape(b, dil, n_sub, DIL_HEADS).transpose(0, 2, 1, 3)
        outs.append(o.reshape(b, t, DIL_HEADS, DIL_HEAD_DIM))
        lses.append(lse.reshape(b, t, DIL_HEADS))
    w = jax.nn.softmax(jnp.stack(lses, axis=-1), axis=-1).astype(q.dtype)
    out = jnp.einsum('bthgd,bthg->bthd', jnp.stack(outs, axis=3), w)
    return out.reshape(b, t, DIL_HEADS * DIL_HEAD_DIM)


def encoder(x, weights):
    (norm_ffn1, w_ffn1_in, w_ffn1_out, norm_mix, w_in, na_rpb, diff_lambda, diff_subln, swa_sink,
     w_branch, w_gate, b_gate, w_out, norm_ffn2, w_ffn2_in, w_ffn2_out, norm_final) = weights
    slopes = alibi_slopes(N_ALIBI)
    n_c = N_DIL_GROUPS * DIL_HEADS
    slopes_d = slopes[:SWA_Q_HEADS].reshape(SWA_KV_HEADS, SWA_GROUP)
    slopes_c = slopes[SWA_Q_HEADS:SWA_Q_HEADS + n_c].reshape(N_DIL_GROUPS, DIL_HEADS)
    slopes_b = slopes[SWA_Q_HEADS + n_c:]
    b, t, d = x.shape
    for l in range(DEPTH):
        x = x + 0.5 * swiglu(rmsnorm(x, norm_ffn1[l]), w_ffn1_in[l], w_ffn1_out[l])
        h = rmsnorm(x, norm_mix[l])
        (q_na, k_na, v_na, q_df, k_df, v_df, q_dl, k_dl, v_dl, q_sw, k_sw, v_sw) = split_cols(h @ w_in[l], IN_SPLITS)
        o_a = neighborhood_attention(q_na.reshape(b, t, NA_HEADS, NA_HEAD_DIM), k_na.reshape(b, t, NA_HEADS, NA_HEAD_DIM),
                                     v_na.reshape(b, t, NA_HEADS, NA_HEAD_DIM), na_rpb[l])
        lam_init = 0.8 - 0.6 * math.exp(-0.3 * l)
        o_b = diff_attention(q_df.reshape(b, t, DIFF_HEADS, 2, DIFF_QK_DIM), k_df.reshape(b, t, DIFF_HEADS, 2, DIFF_QK_DIM),
                             v_df.reshape(b, t, DIFF_HEADS, DIFF_V_DIM), diff_lambda[l], diff_subln[l], slopes_b, lam_init)
        dl_shape = (b, t, N_DIL_GROUPS, DIL_HEADS, DIL_HEAD_DIM)
        o_c = dilated_attention(q_dl.reshape(dl_shape), k_dl.reshape(dl_shape), v_dl.reshape(dl_shape), slopes_c)
        o_d, _ = banded_attention(q_sw.reshape(b, t, SWA_KV_HEADS, SWA_GROUP, SWA_HEAD_DIM),
                                  k_sw.reshape(b, t, SWA_KV_HEADS, SWA_HEAD_DIM), v_sw.reshape(b, t, SWA_KV_HEADS, SWA_HEAD_DIM),
                                  SWA_RADIUS, 1, slopes_d, swa_sink[l].reshape(SWA_KV_HEADS, SWA_GROUP))
        o_d = o_d.reshape(b, t, SWA_Q_W)
        branches = jnp.stack([o_a, o_b, o_c, o_d], axis=2)
        proj = jnp.einsum('btnc,ncd->btnd', branches, w_branch[l])
        gates = jax.nn.sigmoid(h @ w_gate[l] + b_gate[l]).reshape(b, t, N_BRANCHES, d)
        merged = jnp.einsum('btnd,btnd->btd', gates, proj)
        x = x + merged @ w_out[l]
        x = x + 0.5 * swiglu(rmsnorm(x, norm_ffn2[l]), w_ffn2_in[l], w_ffn2_out[l])
    return rmsnorm(x, norm_final)


def setup_inputs(seed: int = 0) -> dict:
    key = jax.random.key(seed)
    ks = jax.random.split(key, 20)
    f32 = jnp.float32

    def normal(k, shape, scale):
        return jax.random.normal(k, shape, f32) * scale

    def gain(k, shape):
        return 1.0 + normal(k, shape, 0.01)

    nl, d = DEPTH, D_MODEL
    return {
        'x_prompt': normal(ks[0], (BATCH, SEQ, d), 1.0),
        'x_sample': normal(ks[1], (DEC_BATCH, DEC_SEQ, d), 1.0),
        'norm_ffn1': gain(ks[2], (nl, d)),
        'w_ffn1_in': normal(ks[3], (nl, d, 2 * D_FF), d ** -0.5),
        'w_ffn1_out': normal(ks[4], (nl, D_FF, d), D_FF ** -0.5),
        'norm_mix': gain(ks[5], (nl, d)),
        'w_in': normal(ks[6], (nl, d, IN_WIDTH), d ** -0.5),
        'na_rpb': normal(ks[7], (nl, NA_HEADS, 2 * NA_ROWS - 1, 2 * NA_COLS - 1), 0.1),
        'diff_lambda': normal(ks[8], (nl, 4, DIFF_QK_DIM), 0.1),
        'diff_subln': gain(ks[9], (nl, DIFF_V_DIM)),
        'swa_sink': normal(ks[10], (nl, SWA_Q_HEADS), 0.5),
        'w_branch': normal(ks[11], (nl, N_BRANCHES, BRANCH_WIDTH, d), BRANCH_WIDTH ** -0.5),
        'w_gate': normal(ks[12], (nl, d, N_BRANCHES * d), d ** -0.5),
        'b_gate': normal(ks[13], (nl, N_BRANCHES * d), 0.02),
        'w_out': normal(ks[14], (nl, d, d), d ** -0.5),
        'norm_ffn2': gain(ks[15], (nl, d)),
        'w_ffn2_in': normal(ks[16], (nl, d, 2 * D_FF), d ** -0.5),
        'w_ffn2_out': normal(ks[17], (nl, D_FF, d), D_FF ** -0.5),
        'norm_final': gain(ks[18], (d,)),
    }


def reference(x_prompt, x_sample, norm_ffn1, w_ffn1_in, w_ffn1_out, norm_mix, w_in, na_rpb, diff_lambda, diff_subln,
              swa_sink, w_branch, w_gate, b_gate, w_out, norm_ffn2, w_ffn2_in, w_ffn2_out, norm_final):
    weights = (norm_ffn1, w_ffn1_in, w_ffn1_out, norm_mix, w_in, na_rpb, diff_lambda, diff_subln, swa_sink,
               w_branch, w_gate, b_gate, w_out, norm_ffn2, w_ffn2_in, w_ffn2_out, norm_final)
    y_prompt = encoder(x_prompt, weights)
    y_sample = encoder(x_sample, weights)
    return (y_prompt, y_sample)
```

```python
import math
import numpy as np
import ml_dtypes
import concourse.bass as bass
import concourse.mybir as mybir
from concourse.bass_utils import run_bass_kernel_spmd

F32 = mybir.dt.float32
BF16 = mybir.dt.bfloat16
AF = mybir.ActivationFunctionType
ALU = mybir.AluOpType

NEG_BIG = -30000.0
DEBUG = {"on": False, "stop": None, "res": None}


class Cfg:
    def __init__(self, D=2048, DFF=5504, DEPTH=2, NTOK=2048):
        self.D = D
        self.DFF = DFF
        self.DEPTH = DEPTH
        self.NTOK = NTOK
        self.NK = 2 * NTOK
        self.KC = D // 128
        self.FC = DFF // 128
        self.TG = 512
        self.NTG = NTOK // 512
        self.NKB = self.NK // 128
        self.INW = 8448
        self.ncores = 8


class Op:
    __slots__ = ("eng", "fn", "deps", "needed", "ms", "dma_key", "dma_val")


COMPUTE = ("pe", "act", "dve", "pool", "sp")


class Prog:
    def __init__(self, same_engine_sync=True):
        self.ops = {e: [] for e in COMPUTE}
        self.last_w = {}
        self.readers = {}
        self.dma_cnt = {}
        self.last_dma = {}
        self.dma_inc = {}
        self.pending = {e: [] for e in COMPUTE}
        self.same_engine_sync = same_engine_sync

    @staticmethod
    def _stream(op):
        return ("d", op.dma_key) if op.dma_key is not None else ("e", op.eng)

    def op(self, eng, fn, reads=(), writes=(), dma_key=None, inc=16):
        o = Op()
        o.eng = eng
        o.fn = fn
        o.needed = False
        o.ms = None
        o.dma_key = dma_key
        o.dma_val = None
        deps = {}

        def add(d, raw=True):
            if d is None:
                return
            if d.dma_key is None and d.eng == eng and dma_key is None:
                if eng == "pe" or not self.same_engine_sync or not raw:
                    return
            s = self._stream(d)
            cur = deps.get(s)
            if cur is None or self._order(d) > self._order(cur):
                deps[s] = d

        for r in reads:
            add(self.last_w.get(r))
        for w in writes:
            add(self.last_w.get(w), raw=False)
            rd = self.readers.get(w)
            if rd:
                for d in rd.values():
                    add(d, raw=False)
        for d in self.pending[eng]:
            add(d)
        self.pending[eng] = []
        o.deps = list(deps.values())
        for d in o.deps:
            d.needed = True
        if dma_key is not None:
            c = self.dma_cnt.get(dma_key, 0) + 1
            self.dma_cnt[dma_key] = c
            o.dma_val = inc * c
            self.dma_inc[dma_key] = inc
            self.last_dma[dma_key] = o
        o.ms = len(self.ops[eng])
        self.ops[eng].append(o)
        for r in reads:
            self.readers.setdefault(r, {})[self._stream(o)] = o
        for w in writes:
            self.last_w[w] = o
            self.readers[w] = {}
        return o

    @staticmethod
    def _order(o):
        return o.dma_val if o.dma_key is not None else o.ms

    def barrier(self):
        toks = []
        for e in COMPUTE:
            for o in reversed(self.ops[e]):
                if o.dma_key is None:
                    toks.append(o)
                    break
        for k, o in self.last_dma.items():
            toks.append(o)
        for e in COMPUTE:
            self.pending[e] = list(toks)

    def replay(self, nc, block, sems, dma_sems):
        for e in COMPUTE:
            c = 0
            for o in self.ops[e]:
                if o.dma_key is None:
                    if o.needed:
                        c += 1
                        o.ms = c
                    else:
                        o.ms = None

        def run(engname):
            def body(e):
                water = {}
                for o in self.ops[engname]:
                    for d in o.deps:
                        if d.dma_key is not None:
                            s, v = dma_sems[d.dma_key], d.dma_val
                        else:
                            s, v = sems[d.eng], d.ms
                        key = id(s)
                        if water.get(key, 0) < v:
                            e.wait_ge(s, v)
                            water[key] = v
                    ins = o.fn(e)
                    if o.dma_key is not None:
                        ins.then_inc(dma_sems[o.dma_key], self.dma_inc[o.dma_key])
                    elif o.needed:
                        ins.then_inc(sems[engname], 1)
                if engname == "sp":
                    for k, c in self.dma_cnt.items():
                        e.wait_ge(dma_sems[k], self.dma_inc[k] * c)
            return body

        block.tensor(run("pe"))
        block.scalar(run("act"))
        block.vector(run("dve"))
        block.gpsimd(run("pool"))
        block.sync(run("sp"))


def alibi_slopes(n):
    return np.exp2(-8.0 * np.arange(1, n + 1, dtype=np.float64) / n)


BANDS = {
    "D": (-1, 4), "C0": (-1, 4), "C1": (-2, 5), "C2": (-8, 11), "B": (-28, 31),
}
RMIN, RMAX = -12, 31


def urange(name):
    lo, hi = BANDS[name]
    return max(lo, RMIN), min(16 + hi, RMAX)


def strip_len(name):
    ulo, uhi = urange(name)
    return 512 + (uhi - ulo) * 128


def band_len(name):
    lo, hi = BANDS[name]
    return 512 + (hi - lo) * 128


def rel_ok(name, rel):
    lo, hi = BANDS[name]
    ulo, uhi = urange(name)
    if rel < ulo or rel > uhi:
        return False
    return (lo <= rel <= hi) or (16 + lo <= rel <= 16 + hi)


def make_strip(slope, uhi, length, rank, radius=None, dil=1):
    p = np.arange(128)[:, None]
    j = np.arange(length)[None, :]
    d = uhi * 128 + p - j - 2048 * rank
    ad = np.abs(d).astype(np.float64)
    v = np.exp(-slope * ad)
    if radius is not None:
        v = np.where(ad <= radius, v, 0.0)
    if dil > 1:
        v = np.where((d % dil) == 0, v, 0.0)
    return v.astype(np.float32)


def host_tables(pair_core, rank):
    sl = alibi_slopes(24)
    t = {}

    def halves(strip):
        out = np.stack([strip, strip], 0)
        if pair_core:
            out[1 - rank] = 0.0
        return out

    def regions(strip):
        out = np.stack([strip, strip, strip], 0)
        if pair_core or rank == 0:
            out[0] = 0.0
        if pair_core or rank == 1:
            out[2] = 0.0
        return out

    lo, hi = BANDS["D"]
    t["stripD"] = np.stack([regions(make_strip(sl[h], hi, band_len("D"), 0, radius=128)) for h in range(8)], 0)
    for gi, (nm, dil) in enumerate((("C0", 1), ("C1", 4), ("C2", 16))):
        lo, hi = BANDS[nm]
        t["strip" + nm] = np.stack(
            [regions(make_strip(sl[8 + gi * 4 + h], hi, band_len(nm), 0, radius=64 * dil, dil=dil))
             for h in range(4)], 0)
    ulo, uhi = urange("B")
    t["stripB"] = np.stack([halves(make_strip(sl[20 + h], uhi, strip_len("B"), rank)) for h in range(4)], 0)
    kc = np.arange(64)[:, None]
    qc = np.arange(64)[None, :]
    dc = np.clip(kc - qc + 15, 0, 30)
    ic = np.zeros((32, 64, 64), np.float32)
    for b in range(31):
        ic[b] = (dc == b)
    c0 = np.clip(qc - 8, 0, 48)
    colok = (kc >= c0) & (kc < c0 + 16)
    ic[31] = np.where(colok, 0.0, NEG_BIG)
    t["na_ic"] = ic.reshape(32, 4096)
    mt = np.full((2, 4, 8, 8), NEG_BIG, np.float32)
    seq_rows = 32 if pair_core else 64
    for g in range(4):
        for j in range(8):
            rb = 4 * g - 2 + j
            if rb < 0 and (pair_core or rank == 0):
                continue
            if rb >= 16 and (pair_core or rank == 1):
                continue
            for qr8 in range(8):
                qr = 8 * (g + 4 * rank) + qr8
                s0 = (qr // seq_rows) * seq_rows
                r0 = s0 + min(max(qr - s0 - 4, 0), seq_rows - 8)
                for kr2 in range(2):
                    kr = 2 * (16 * rank + rb) + kr2
                    if r0 <= kr < r0 + 8:
                        mt[kr2, g, j, qr8] = 0.0
    t["na_mt"] = mt.reshape(2, 256)
    kr = np.zeros((2, 128), np.float32)
    kr[0, 0:64] = 1.0
    kr[1, 64:128] = 1.0
    t["na_kr"] = kr
    return t


def rpb_layout(na_rpb):
    L = na_rpb.shape[0]
    out = np.zeros((L, 8, 32, 23), np.float32)
    for a in range(15):
        out[:, :, 0:31, a + 4] = na_rpb[:, :, 14 - a, :]
    out[:, :, 31, :] = 1.0
    return out


def build_program(cfg):
    D, DFF, L, NTOK, KC, FC = cfg.D, cfg.DFF, cfg.DEPTH, cfg.NTOK, cfg.KC, cfg.FC
    NTG, NKB, INW = cfg.NTG, cfg.NKB, cfg.INW
    nc = bass.Bass("TRN2", target_bir_lowering=False)
    P = Prog()

    def din(name, shape, dt=F32):
        return nc.dram_tensor(name, list(shape), dt, kind="ExternalInput").ap()

    x_in = din("x", [NTOK, D])
    y_out = nc.dram_tensor("y", [NTOK, D], F32, kind="ExternalOutput").ap()
    norm_ffn1 = din("norm_ffn1", [L, D])
    w_ffn1_in = din("w_ffn1_in", [L, D, 2 * DFF])
    w_ffn1_out = din("w_ffn1_out", [L, DFF, D])
    norm_mix = din("norm_mix", [L, D])
    w_in = din("w_in", [L, D, INW])
    rpbT = din("rpbT", [L, 8, 32, 23])
    diff_lambda = din("diff_lambda", [L, 256])
    diff_subln = din("diff_subln", [L, 128])
    swa_sink = din("swa_sink", [L, 8])
    w_branch = din("w_branch", [L, 4, 512, D])
    w_gate = din("w_gate", [L, D, 4 * D])
    b_gate = din("b_gate", [L, 4 * D])
    w_out = din("w_out", [L, D, D])
    norm_ffn2 = din("norm_ffn2", [L, D])
    w_ffn2_in = din("w_ffn2_in", [L, D, 2 * DFF])
    w_ffn2_out = din("w_ffn2_out", [L, DFF, D])
    norm_final = din("norm_final", [1, D])
    stripD = din("stripD", [8, 3, 128, band_len("D")], BF16)
    stripC = [din("stripC%d" % g, [4, 3, 128, band_len("C%d" % g)], BF16) for g in range(3)]
    stripB = din("stripB", [4, 2, 128, strip_len("B")], BF16)
    na_ic = din("na_ic", [32, 4096])
    na_mt = din("na_mt", [2, 256])
    na_kr = din("na_kr", [2, 128])

    skind = "ExternalOutput" if DEBUG["on"] else "Internal"
    XT = nc.dram_tensor("XT", [KC, 128, NTOK], F32, kind=skind).ap()
    QKT = nc.dram_tensor("QKT", [INW, NTOK], BF16, kind=skind).ap()
    NKV = 2688
    NK = cfg.NK
    KVP = [(2560, 128), (1024, 512), (1536, 512), (2048, 512), (512, 512), (0, 512)]
    KTo = {o: nc.dram_tensor("KTo%d" % o, [w, NTOK], BF16).ap() for o, w in KVP}
    VTo = {o: nc.dram_tensor("VTo%d" % o, [NTOK, w], BF16).ap() for o, w in KVP}
    KTa = [{o: nc.dram_tensor("KTa%d_%d" % (l, o), [2 * w, NTOK], BF16).ap() for o, w in KVP} for l in range(L)]
    VTa = [{o: nc.dram_tensor("VTa%d_%d" % (l, o), [NK, w], BF16).ap() for o, w in KVP} for l in range(L)]

    def panel_of(off):
        po = max(o for o, w in KVP if o <= off)
        return po, off - po
    GROUPS = [[2 * i, 2 * i + 1] for i in range(cfg.ncores // 2)]
    BRT = nc.dram_tensor("BRT", [2048, NTOK], BF16, kind=skind).ap()
    HX = nc.dram_tensor("HX", [8, 23, 64, 64], BF16).ap()

    TOTAL = 206 * 1024
    arena = nc.alloc_sbuf_tensor("arena", [128, TOTAL // 4], F32)
    cursor = [0]

    def carve(nbytes):
        off = cursor[0]
        cursor[0] += (nbytes + 63) // 64 * 64
        assert cursor[0] <= TOTAL, ("SBUF overflow", cursor[0])
        return off

    def view(off, nbytes, dt=F32, pat=None, **kw):
        a = arena[:, off // 4:(off + nbytes) // 4]
        if dt != F32:
            a = a.bitcast(dt)
        if pat is not None:
            a = a.rearrange(pat, **kw)
        return a

    o_ident = carve(512)
    ident = view(o_ident, 512)
    o_onesf = carve(512)
    ones_f = view(o_onesf, 512)
    o_onesb = carve(256)
    ones_b = view(o_onesb, 256, BF16)
    NG = 3 * L + 1
    o_gain = carve(NG * KC * 4)
    gains = view(o_gain, NG * KC * 4)
    o_bg = carve(L * 4 * KC * 4)
    bgate = view(o_bg, L * 4 * KC * 4)
    o_small = carve(64 * 4)
    small = view(o_small, 64 * 4)
    o_dl = carve(256 * 4)
    dl = view(o_dl, 256 * 4)
    o_mt = carve(512)
    mt_sb = view(o_mt, 512, BF16)
    o_kr = carve(256)
    kr_sb = view(o_kr, 256, BF16)
    NW = 4
    o_w = [carve(16 * 512 * 2) for _ in range(NW)]
    wslots = [view(o, 16 * 512 * 2, BF16, "p (k n) -> p k n", n=512) for o in o_w]
    base = cursor[0]

    cursor[0] = base
    o_xT = carve(KC * 512 * 4)
    xT = view(o_xT, KC * 512 * 4, F32, "p (c t) -> p c t", t=512)
    o_hT = carve(KC * 512 * 2)
    hT = view(o_hT, KC * 512 * 2, BF16, "p (c t) -> p c t", t=512)
    NAR = max(FC, 32)
    o_ar = carve(NAR * 512 * 2)
    ar = view(o_ar, NAR * 512 * 2, BF16, "p (c t) -> p c t", t=512)
    o_macc = carve(4 * 512 * 4)
    macc = view(o_macc, 4 * 512 * 4, F32, "p (c t) -> p c t", t=512)
    tmpf = []
    for i in range(4):
        o = carve(2048)
        tmpf.append(view(o, 2048))
    stg = []
    for i in range(4):
        o = carve(1024)
        stg.append(view(o, 1024, BF16))
    o_rstd = carve(2048)
    rstd = view(o_rstd, 2048)
    xstage = [view(o_ar + i * D * 4, D * 4) for i in range(2)]
    assert 2 * D * 4 <= NAR * 512 * 2
    tok_end = cursor[0]

    cursor[0] = base
    QT = []
    KTb = []
    for i in range(3):
        o = carve(NTOK * 2)
        QT.append(view(o, NTOK * 2, BF16))
    for i in range(3):
        o = carve(NK * 2)
        KTb.append(view(o, NK * 2, BF16))
    o_V = carve(NKB * 384 * 2)
    Vb = view(o_V, NKB * 384 * 2, BF16, "p (k c) -> p k c", c=384)
    SBYTES = max(2 * strip_len("B") * 2, 2 * 3 * sum(band_len("C%d" % g) for g in range(3)))
    o_S = carve(SBYTES)
    NE, NP_, LA = 2, 4, 2
    Eb = []
    for i in range(NE):
        o = carve(4096)
        Eb.append(view(o, 4096))
    Pb = []
    for i in range(NP_):
        o = carve(2048)
        Pb.append(view(o, 2048, BF16))

    att_f = []
    for i in range(6):
        o = carve(2048)
        att_f.append(view(o, 2048))
    ob = []
    for i in range(2):
        o = carve(1024)
        ob.append(view(o, 1024, BF16))
    att_end = cursor[0]
    assert max(tok_end, att_end) <= TOTAL

    pd = [nc.alloc_psum_tensor("pd%d" % i, [128, 1024], F32).ap() for i in range(4)]
    ps = [pd[i // 2][:, (i % 2) * 512:(i % 2 + 1) * 512] for i in range(8)]

    def mm(out, lhsT, rhs, start, stop, reads, writes):
        return P.op("pe", lambda e: e.matmul(out, lhsT=lhsT, rhs=rhs, start=start, stop=stop), reads, writes)

    def act(out, in_, func, reads, writes, bias=None, scale=None):
        kw = {}
        if bias is not None:
            kw["bias"] = bias
        if scale is not None:
            kw["scale"] = scale
        return P.op("act", lambda e: e.activation(out, in_, func, **kw), reads, writes)

    def tt(eng, out, in0, in1, op, reads, writes):
        return P.op(eng, lambda e: e.tensor_tensor(out, in0, in1, op), reads, writes)

    def stt(eng, out, in0, scalar, in1, op0, op1, reads, writes):
        return P.op(eng, lambda e: e.scalar_tensor_tensor(out, in0, scalar, in1, op0, op1), reads, writes)

    def ts(eng, out, in0, s1, s2, op0, op1, reads, writes):
        if s2 is None:
            return P.op(eng, lambda e: e.tensor_scalar(out, in0, s1, None, op0), reads, writes)
        return P.op(eng, lambda e: e.tensor_scalar(out, in0, s1, s2, op0, op1), reads, writes)

    def recip(out, in_, reads, writes):
        return P.op("dve", lambda e: e.reciprocal(out, in_), reads, writes)

    def copy(eng, out, in_, reads, writes):
        return P.op(eng, lambda e: e.tensor_copy(out, in_), reads, writes)

    def dma(eng, out, in_, key, reads, writes, slow=False):
        if slow:
            return P.op(eng, lambda e: e.dma_start(out=out, in_=in_, allow_slow_non_contiguous=True),
                        reads, writes, dma_key=key)
        return P.op(eng, lambda e: e.dma_start(out=out, in_=in_), reads, writes, dma_key=key)

    wcnt = [0]

    def wload(src, nk, ncol):
        s = wcnt[0] % NW
        wcnt[0] += 1
        dma("pool", wslots[s][:, 0:nk, 0:ncol], src, "w%d" % s, [], [("w", s)])
        return wslots[s], ("w", s)

    def wview(w2d):
        return w2d.rearrange("(kc p) n -> p kc n", p=128)

    P.op("pool", lambda e: e.memset(ident, 0.0), [], ["ident"])
    P.op("pool", lambda e: e.affine_select(out=ident, in_=ident, pattern=[[-1, 128]], compare_op=ALU.not_equal,
                                           fill=1.0, base=0, channel_multiplier=1), ["ident"], ["ident"])
    P.op("pool", lambda e: e.memset(ones_f, 1.0), [], ["ones_f"])
    P.op("pool", lambda e: e.memset(ones_b, 1.0), [], ["ones_b"])
    gsrc = [norm_ffn1, norm_mix, norm_ffn2]
    for l in range(L):
        for i in range(3):
            gi = l * 3 + i
            dma("sp", gains[:, gi * KC:(gi + 1) * KC], gsrc[i][l].rearrange("(c p) -> p c", p=128), "setup",
                [], [("gains", gi)], slow=True)
        dma("sp", bgate[:, l * 4 * KC:(l + 1) * 4 * KC], b_gate[l].rearrange("(c p) -> p c", p=128), "setup",
            [], [("bgate", l)], slow=True)
    dma("sp", gains[:, 3 * L * KC:(3 * L + 1) * KC], norm_final[0].rearrange("(c p) -> p c", p=128), "setup",
        [], [("gains", 3 * L)], slow=True)
    dma("pool", mt_sb[0:2, :], na_mt, "setup2", [], ["natt_c"])
    dma("pool", kr_sb[0:2, :], na_kr, "setup2", [], ["natt_c2"])
    CONST_R = ["ident", "ones_f", "ones_b", "gains", "bgate", "rv"]
    P.barrier()

    def gain_col(gi, c):
        return gains[:, gi * KC + c:gi * KC + c + 1]

    def R_x(c):
        return ("xT", c)

    def R_h(c):
        return ("hT", c)

    def R_ar(c):
        return ("ar", c)

    def load_x_input(tg):
        for tbl in range(4):
            tb = tg * 4 + tbl
            st = xstage[tbl % 2]
            rs = ("xstage", tbl % 2)
            dma("sp", st, x_in[tb * 128:(tb + 1) * 128, :], "xs%d" % (tbl % 2), [], [rs] + [R_ar(i) for i in range(NAR)])
            for cg in range(KC // 4):
                bank = (tbl * (KC // 4) + cg) % 4
                for ci in range(4):
                    c = cg * 4 + ci
                    P.op("pe", (lambda o, i: (lambda e: e.transpose(o, i, ident)))(ps[bank][:, ci * 128:(ci + 1) * 128],
                                                                                 st[:, c * 128:(c + 1) * 128]),
                         [rs], [("ps", bank)])
                eng = "dve" if cg % 2 == 0 else "act"
                o_ap = xT[:, cg * 4:(cg + 1) * 4, tbl * 128:(tbl + 1) * 128]
                i_ap = ps[bank].rearrange("p (c t) -> p c t", t=128)
                if eng == "dve":
                    copy("dve", o_ap, i_ap, [("ps", bank)], [R_x(cg * 4 + i) for i in range(4)])
                else:
                    P.op("act", (lambda o, i: (lambda e: e.copy(o, i)))(o_ap, i_ap), [("ps", bank)],
                         [R_x(cg * 4 + i) for i in range(4)])

    def load_x_scratch(tg):
        dma("sp", xT, XT.rearrange("c p t -> p c t")[:, :, tg * 512:(tg + 1) * 512], "xl", [("XT", tg)],
            [R_x(c) for c in range(KC)])

    def store_x_scratch(tg):
        dma("sp", XT.rearrange("c p t -> p c t")[:, :, tg * 512:(tg + 1) * 512], xT, "xst",
            [R_x(c) for c in range(KC)], [("XT", tg)])

    def norm_stats(src_fn, nchunks, nrows, divisor, eps, bank):
        for c in range(nchunks):
            src, rres = src_fn(c)
            sq = tmpf[c % 2]
            act(sq[0:nrows, :], src, AF.Square, [rres], [("tmpf", c % 2)])
            mm(ps[bank], ones_f[0:nrows, :], sq[0:nrows, :], c == 0, c == nchunks - 1, [("tmpf", c % 2), "ones_f"],
               [("ps", bank)])
        act(tmpf[2], ps[bank], AF.Sqrt, [("ps", bank)], [("tmpf", 2)], bias=float(eps), scale=1.0 / divisor)
        recip(rstd, tmpf[2], [("tmpf", 2)], ["rstd"])

    def norm_h(gi):
        norm_stats(lambda c: (xT[:, c, :], R_x(c)), KC, 128, float(D), 1e-6, 7)
        for c in range(KC):
            stt("dve", hT[:, c, :], xT[:, c, :], gain_col(gi, c), rstd, ALU.mult, ALU.mult,
                [R_x(c), "rstd", "gains"], [R_h(c)])

    def ffn(l, w_inp, w_outp, gi):
        norm_h(gi)
        wv = wview(w_inp[l])
        npan = (FC + 3) // 4
        for fp in range(npan):
            nf = min(4, FC - 4 * fp)
            wg, rg = wload(wv[:, :, fp * 512:fp * 512 + nf * 128], KC, nf * 128)
            wu, ru = wload(wv[:, :, DFF + fp * 512:DFF + fp * 512 + nf * 128], KC, nf * 128)
            for j in range(nf):
                f = fp * 4 + j
                ba, bb = 2 * (f % 2), 2 * (f % 2) + 1
                for k in range(KC):
                    mm(ps[ba], wg[:, k, j * 128:(j + 1) * 128], hT[:, k, :], k == 0, k == KC - 1, [rg, R_h(k)],
                       [("ps", ba)])
                for k in range(KC):
                    mm(ps[bb], wu[:, k, j * 128:(j + 1) * 128], hT[:, k, :], k == 0, k == KC - 1, [ru, R_h(k)],
                       [("ps", bb)])
                t = tmpf[f % 2]
                act(t, ps[ba], AF.Silu, [("ps", ba)], [("tmpf", f % 2)])
                tt("dve", ar[:, f, :], t, ps[bb], ALU.mult, [("tmpf", f % 2), ("ps", bb)], [R_ar(f)])
        wo = wview(w_outp[l])
        nfb = (FC + 15) // 16
        for dgp in range(KC // 4):
            b0 = 4 * (dgp % 2)
            for fb in range(nfb):
                nfk = min(16, FC - 16 * fb)
                wt, rw = wload(wo[:, fb * 16:fb * 16 + nfk, dgp * 512:(dgp + 1) * 512], nfk, 512)
                for fi in range(nfk):
                    f = fb * 16 + fi
                    for j in range(4):
                        mm(ps[b0 + j], wt[:, fi, j * 128:(j + 1) * 128], ar[:, f, :], f == 0, f == FC - 1,
                           [rw, R_ar(f)], [("ps", b0 + j)])
            for j in range(4):
                c = dgp * 4 + j
                stt("dve", xT[:, c, :], ps[b0 + j], 0.5, xT[:, c, :], ALU.mult, ALU.add, [("ps", b0 + j), R_x(c)],
                    [R_x(c)])

    PANELS = [(0, 512, "q", 0.125, 0), (512, 512, "k", 1.0, 0), (1024, 512, "v", 1.0, 0),
              (1536, 512, "q", 0.125, 1536), (2048, 512, "k", 1.0, 512), (2560, 512, "v", 1.0, 512)]
    for g in range(3):
        PANELS.append((3072 + g * 512, 512, "q", 128 ** -0.5, 3072 + g * 512))
    for g in range(3):
        PANELS.append((4608 + g * 512, 512, "k", 1.0, 1024 + g * 512))
    for g in range(3):
        PANELS.append((6144 + g * 512, 512, "v", 1.0, 1024 + g * 512))
    PANELS += [(7680, 512, "q", 0.125, 7680), (8192, 128, "k", 1.0, 2560), (8320, 128, "v", 1.0, 2560)]
    scnt = [0]

    def qkv(l, tg):
        norm_h(l * 3 + 1)
        wv = wview(w_in[l])
        pcnt = 0
        for (c0, ncol, kind, scale, dst) in PANELS:
            wt, rw = wload(wv[:, :, c0:c0 + ncol], KC, ncol)
            if kind != "v":
                for j in range(ncol // 128):
                    bank = pcnt % 4
                    pcnt += 1
                    for k in range(KC):
                        mm(ps[bank], wt[:, k, j * 128:(j + 1) * 128], hT[:, k, :], k == 0, k == KC - 1,
                           [rw, R_h(k)], [("ps", bank)])
                    s = scnt[0] % 4
                    scnt[0] += 1
                    if s % 2 == 0:
                        act(stg[s], ps[bank], AF.Copy, [("ps", bank)], [("stg", s)], scale=float(scale))
                    else:
                        ts("dve", stg[s], ps[bank], float(scale), None, ALU.mult, None, [("ps", bank)], [("stg", s)])
                    if kind == "q":
                        dap = QKT[dst + j * 128:dst + (j + 1) * 128, tg * 512:(tg + 1) * 512]
                    else:
                        dap = KTo[dst][j * 128:(j + 1) * 128, tg * 512:(tg + 1) * 512]
                    dma("sp", dap, stg[s], "stg%d" % s, [("stg", s)], [])
            else:
                for tbl in range(4):
                    bank = pcnt % 4
                    pcnt += 1
                    for k in range(KC):
                        mm(ps[bank][:, 0:ncol], hT[:, k, tbl * 128:(tbl + 1) * 128], wt[:, k, 0:ncol], k == 0,
                           k == KC - 1, [rw, R_h(k)], [("ps", bank)])
                    s = scnt[0] % 4
                    scnt[0] += 1
                    if s % 2 == 0:
                        act(stg[s][:, 0:ncol], ps[bank][:, 0:ncol], AF.Copy, [("ps", bank)], [("stg", s)])
                    else:
                        copy("dve", stg[s][:, 0:ncol], ps[bank][:, 0:ncol], [("ps", bank)], [("stg", s)])
                    r0 = tg * 512 + tbl * 128
                    dma("sp", VTo[dst][r0:r0 + 128, 0:ncol], stg[s][:, 0:ncol], "stg%d" % s, [("stg", s)], [])

    def post(l, tg):
        norm_h(l * 3 + 1)
        brT = ar
        dma("sp", ar[:, 0:16, :], BRT.rearrange("(c p) t -> p c t", p=128)[:, :, tg * 512:(tg + 1) * 512], "brl", [],
            [R_ar(c) for c in range(16)])
        wg_v = wview(w_gate[l])
        for dgp in range(KC // 4):
            for n in range(4):
                wgt, rg = wload(wg_v[:, :, n * D + dgp * 512:n * D + (dgp + 1) * 512], KC, 512)
                wbr, rb = wload(wview(w_branch[l, n])[:, :, dgp * 512:(dgp + 1) * 512], 4, 512)
                for j in range(4):
                    dc = dgp * 4 + j
                    pg, pp = 2 * (j % 2), 2 * (j % 2) + 1
                    for k in range(KC):
                        mm(ps[pg], wgt[:, k, j * 128:(j + 1) * 128], hT[:, k, :], k == 0, k == KC - 1, [rg, R_h(k)],
                           [("ps", pg)])
                    for c in range(4):
                        mm(ps[pp], wbr[:, c, j * 128:(j + 1) * 128], brT[:, n * 4 + c, :], c == 0, c == 3,
                           [rb, R_ar(n * 4 + c)], [("ps", pp)])
                    sg = tmpf[j % 2]
                    bcol = bgate[:, l * 4 * KC + n * KC + dc:l * 4 * KC + n * KC + dc + 1]
                    act(sg, ps[pg], AF.Sigmoid, [("ps", pg), "bgate"], [("tmpf", j % 2)], bias=bcol)
                    if n == 0:
                        tt("dve", macc[:, j, :], sg, ps[pp], ALU.mult, [("tmpf", j % 2), ("ps", pp)], [("macc", j)])
                    else:
                        t2 = tmpf[2 + j % 2]
                        tt("dve", t2, sg, ps[pp], ALU.mult, [("tmpf", j % 2), ("ps", pp)], [("tmpf", 2 + j % 2)])
                        if n < 3:
                            tt("dve", macc[:, j, :], macc[:, j, :], t2, ALU.add, [("macc", j), ("tmpf", 2 + j % 2)],
                               [("macc", j)])
                        else:
                            tt("dve", ar[:, 16 + dc, :], macc[:, j, :], t2, ALU.add,
                               [("macc", j), ("tmpf", 2 + j % 2)], [R_ar(16 + dc)])
        wo_v = wview(w_out[l])
        for dgp in range(KC // 4):
            wt, rw = wload(wo_v[:, :, dgp * 512:(dgp + 1) * 512], KC, 512)
            for j in range(4):
                c = dgp * 4 + j
                bank = 4 + (c % 4)
                for k in range(KC):
                    mm(ps[bank], wt[:, k, j * 128:(j + 1) * 128], ar[:, 16 + k, :], k == 0, k == KC - 1,
                       [rw, R_ar(16 + k)], [("ps", bank)])
                tt("dve", xT[:, c, :], ps[bank], xT[:, c, :], ALU.add, [("ps", bank), R_x(c)], [R_x(c)])

    def final_out(tg):
        gi = 3 * L
        norm_stats(lambda c: (xT[:, c, :], R_x(c)), KC, 128, float(D), 1e-6, 7)
        for c in range(KC):
            stt("dve", xT[:, c, :], xT[:, c, :], gain_col(gi, c), rstd, ALU.mult, ALU.mult,
                [R_x(c), "rstd", "gains"], [R_x(c)])
        for tbl in range(4):
            st = xstage[tbl % 2]
            rs = ("xstage", tbl % 2)
            for cg in range(KC // 4):
                bank = (tbl * (KC // 4) + cg) % 4
                for ci in range(4):
                    c = cg * 4 + ci
                    P.op("pe", (lambda o, i: (lambda e: e.transpose(o, i, ident)))(ps[bank][:, ci * 128:(ci + 1) * 128],
                                                                                 xT[:, c, tbl * 128:(tbl + 1) * 128]),
                         [R_x(c)], [("ps", bank)])
                o_ap = st[:, cg * 512:(cg + 1) * 512]
                if cg % 2 == 0:
                    copy("dve", o_ap, ps[bank], [("ps", bank)], [rs] + [R_ar(i) for i in range(NAR)])
                else:
                    P.op("act", (lambda o, i: (lambda e: e.copy(o, i)))(o_ap, ps[bank]), [("ps", bank)],
                         [rs] + [R_ar(i) for i in range(NAR)])
            r0 = tg * 512 + tbl * 128
            dma("sp", y_out[r0:r0 + 128, :], st, "yo%d" % (tbl % 2), [rs] + [R_ar(i) for i in range(NAR)], [])

    acnt = {"s": 0, "e": 0, "p": 0, "o": 0}

    pend = []
    SVv = view(o_S, SBYTES, BF16)
    STRIP_R = ["strips"] + [("T1", k) for k in range(15)]

    def flush(keep=0):
        while len(pend) > keep:
            pend.pop(0)()

    def sv_ap(o1, o2=None):
        base = SVv[:, o1:o1 + 512]
        if o2 is None:
            return base
        return bass.AP(base.tensor, base.offset, [list(base.ap[0]), [o2 - o1, 2], [1, 512]])

    def attn_group(items, psO, psZ, dv, epilogue):
        n = len(items)
        idx = 0
        while idx < n:
            grp = [idx] if idx + 1 >= n else [idx, idx + 1]
            if len(grp) == 2 and items[grp[1]][3] < items[grp[0]][3]:
                grp = [grp[1], grp[0]]
            idx += len(grp)
            sp = acnt["s"] % 2
            acnt["s"] += 1
            banks = [2 * sp, 2 * sp + 1][:len(grp)]
            for t, ii in enumerate(grp):
                kt_ap, qt_ap, v_ap, svo, rds, extra = items[ii]
                if extra is None:
                    mm(ps[banks[t]], kt_ap, qt_ap, True, True, rds, [("ps", banks[t])])
                else:
                    mm(ps[banks[t]], kt_ap, qt_ap, True, False, rds, [("ps", banks[t])])
                    mm(ps[banks[t]], kr_sb[0:2, :], extra, False, True, ["natt_c"], [("ps", banks[t])])
            eb = acnt["e"] % NE
            acnt["e"] += 1
            pb = acnt["p"] % NP_
            acnt["p"] += 1
            rps = [("ps", b) for b in banks]
            if len(grp) == 2:
                act(Eb[eb], pd[sp], AF.Exp, rps, [("E", eb)])
                tt("dve", Pb[pb].rearrange("p (t c) -> p t c", c=512), Eb[eb].rearrange("p (t c) -> p t c", c=512),
                   sv_ap(items[grp[0]][3], items[grp[1]][3]), ALU.mult, [("E", eb)] + STRIP_R, [("P", pb)])
            else:
                act(Eb[eb][:, 0:512], ps[banks[0]], AF.Exp, rps, [("E", eb)])
                tt("dve", Pb[pb][:, 0:512], Eb[eb][:, 0:512], sv_ap(items[grp[0]][3]), ALU.mult,
                   [("E", eb)] + STRIP_R, [("P", pb)])

            def pv(grp=grp, pb=pb):
                for t, ii in enumerate(grp):
                    v_ap, rds = items[ii][2], items[ii][4]
                    first, last = ii == min(grp) and min(grp) == 0, ii == max(grp) and max(grp) == n - 1
                    st = (t == 0) and (0 in grp)
                    sp_ = (t == len(grp) - 1) and ((n - 1) in grp)
                    pslice = Pb[pb][:, t * 512:(t + 1) * 512]
                    mm(ps[psO][0:dv, :], v_ap, pslice, st, sp_, [("P", pb)] + rds, [("ps", psO)])
                    mm(ps[psZ], ones_b, pslice, st, sp_, [("P", pb), "ones_b"], [("ps", psZ)])
                if (n - 1) in grp:
                    epilogue()
            pend.append(pv)
            flush(LA)

    def store_branch(src, row0, nrows, qg, res):
        dma("sp", BRT[row0:row0 + nrows, qg * 512:(qg + 1) * 512], src, "bro%d" % res[1], [res], [])

    def khalf(kb):
        return kb // 16

    def load_q(dst, row0, nrows, key, res):
        dma("sp", dst[0:nrows, :], QKT[row0:row0 + nrows, :], key, [], [res])

    def KTR(i):
        return [("KT", i), ("KT", i, "a"), ("KT", i, "b")]

    VR = ["V"] + [("V", g_, x_) for g_ in range(3) for x_ in "abc"]

    def load_k(l, dst, row0, nrows, key, res):
        po, lr = panel_of(row0)
        src = KTa[l][po].rearrange("(r f) t -> f r t", r=2)[lr:lr + nrows, :, :]
        dma("sp", dst[0:nrows, :].rearrange("p (r t) -> p r t", r=2), src, key, [("KTa", po)], KTR(res[1]))

    def load_v(l, dst, col0, ncols):
        po, lc = panel_of(col0)
        dma("sp", dst, VTa[l][po][:, lc:lc + ncols].rearrange("(k p) c -> p k c", p=128), "vb", [("VTa", po)], VR)

    def load_k_rel(l, dst, row0, nrows, key, res):
        po, lr = panel_of(row0)
        w = KTo[po].shape[0]
        i = res[1]
        dma("sp", dst[0:nrows, 0:1024], KTa[l][po][lr:lr + nrows, 1024:2048], key, [("KTa", po)], [("KT", i, "a")])
        dma("sp", dst[0:nrows, 1024:3072], KTo[po][lr:lr + nrows, :], key, [], [("KT", i, "b")])
        dma("sp", dst[0:nrows, 3072:4096], KTa[l][po][w + lr:w + lr + nrows, 0:1024], key, [("KTa", po)],
            [("KT", i)])

    def load_v_rel(l, dst, col0, ncols, tag=0, last=True):
        po, lc = panel_of(col0)
        dma("sp", dst[:, 0:8, :], VTa[l][po][1024:2048, lc:lc + ncols].rearrange("(k p) c -> p k c", p=128), "vb",
            [("VTa", po)], [("V", tag, "a")])
        dma("sp", dst[:, 8:24, :], VTo[po][:, lc:lc + ncols].rearrange("(k p) c -> p k c", p=128), "vb", [],
            [("V", tag, "b")])
        dma("sp", dst[:, 24:32, :], VTa[l][po][2048:3072, lc:lc + ncols].rearrange("(k p) c -> p k c", p=128), "vb",
            [("VTa", po)], ["V" if last else ("V", tag, "c")])

    def cc_gather(src, dst, res):
        P.op("pool", lambda e: e.collective_compute("AllGather", ALU.bypass, replica_groups=GROUPS,
                                                    ins=[src.opt()], outs=[dst.opt()]),
             [], ["ccres", res], dma_key="cc", inc=1)

    def exchange(l):
        P.barrier()
        for o, w in KVP:
            cc_gather(KTo[o], KTa[l][o], ("KTa", o))
            cc_gather(VTo[o], VTa[l][o], ("VTa", o))

    def band_tiles_rel(name, qg):
        lo, hi = BANDS[name]
        out = []
        for rel in range(lo, hi + 1):
            rb = 4 * qg + rel
            assert -8 <= rb < 24
            out.append((rb + 8, 0 if rb < 0 else (1 if rb < 16 else 2), (hi - rel) * 128))
        return out

    def band_tiles(name, qg):
        ulo, uhi = urange(name)
        out = []
        for kb in range(NKB):
            rel = kb - 4 * qg
            if rel_ok(name, rel):
                out.append((kb, (uhi - rel) * 128))
        return out

    def attention(l):
        SV = view(o_S, SBYTES, BF16)

        SL = band_len("D")
        sD = SV[:, 0:3 * SL].rearrange("p (s j) -> p s j", j=SL)
        esink = small[:, 8:16]
        dma("sp", small[:, 0:8], swa_sink[l].partition_broadcast(128), "m_sink", [], ["small"])
        act(esink, small[:, 0:8], AF.Exp, ["small"], ["esink"])
        for hk in range(2):
            flush()
            load_k_rel(l, KTb[0], 2560 + hk * 64, 64, "kt0", ("KT", 0))
            load_v_rel(l, Vb[:, :, 0:64], 2560 + hk * 64, 64)
            for gq in range(4):
                hd = hk * 4 + gq
                load_q(QT[0], 7680 + hd * 64, 64, "qt0", ("QT", 0))
                dma("sp", sD, stripD[hd].rearrange("s p j -> p s j"), "strips", [], STRIP_R)
                for qg in range(NTG):
                    items = []
                    for kb, reg, off in band_tiles_rel("D", qg):
                        items.append((KTb[0][0:64, kb * 128:(kb + 1) * 128], QT[0][0:64, qg * 512:(qg + 1) * 512],
                                      Vb[:, kb, 0:64], reg * SL + off,
                                      KTR(0) + [("QT", 0)] + VR, None))
                    oi = acnt["o"] % 2
                    acnt["o"] += 1

                    def epi(oi=oi, hd=hd, qg=qg):
                        pO, pZ = 4 + oi, 6 + oi
                        zt = att_f[oi]
                        act(zt[0:64, :], ps[pZ][0:64, :], AF.Ln, [("ps", pZ), "esink"], [("af", oi)],
                            bias=esink[0:64, hd:hd + 1])
                        act(zt[0:64, :], zt[0:64, :], AF.Exp, [("af", oi)], [("af", oi)], scale=-1.0)
                        tt("dve", ob[oi][0:64, :], ps[pO][0:64, :], zt[0:64, :], ALU.mult, [("ps", pO), ("af", oi)],
                           [("ob", oi)])
                        store_branch(ob[oi][0:64, :], 1536 + hd * 64, 64, qg, ("ob", oi))
                    attn_group(items, 4 + oi, 6 + oi, 64, epi)

        offs = [0]
        for g in range(3):
            offs.append(offs[-1] + 3 * band_len("C%d" % g))
        sC = [SV[:, offs[g]:offs[g + 1]].rearrange("p (s j) -> p s j", j=band_len("C%d" % g)) for g in range(3)]
        for h in range(4):
            flush()
            for g in range(3):
                load_q(QT[g], 3072 + g * 512 + h * 128, 128, "qt%d" % g, ("QT", g))
                load_k_rel(l, KTb[g], 1024 + g * 512 + h * 128, 128, "kt%d" % g, ("KT", g))
                load_v_rel(l, Vb[:, :, g * 128:(g + 1) * 128], 1024 + g * 512 + h * 128, 128, tag=g, last=(g == 2))
                dma("sp", sC[g], stripC[g][h].rearrange("s p j -> p s j"), "strips", [],
                    ["strips" if g == 2 else ("T1", g)])
            for qg in range(NTG):
                items = []
                for g in range(3):
                    for kb, reg, off in band_tiles_rel("C%d" % g, qg):
                        items.append((KTb[g][:, kb * 128:(kb + 1) * 128], QT[g][:, qg * 512:(qg + 1) * 512],
                                      Vb[:, kb, g * 128:(g + 1) * 128],
                                      offs[g] + reg * band_len("C%d" % g) + off,
                                      KTR(g) + [("QT", g)] + VR, None))
                oi = acnt["o"] % 2
                acnt["o"] += 1

                def epi(oi=oi, h=h, qg=qg):
                    pO, pZ = 4 + oi, 6 + oi
                    zt = att_f[oi]
                    act(zt, ps[pZ], AF.Ln, [("ps", pZ)], [("af", oi)])
                    act(zt, zt, AF.Exp, [("af", oi)], [("af", oi)], scale=-1.0)
                    tt("dve", ob[oi], ps[pO], zt, ALU.mult, [("ps", pO), ("af", oi)], [("ob", oi)])
                    store_branch(ob[oi], 1024 + h * 128, 128, qg, ("ob", oi))
                attn_group(items, 4 + oi, 6 + oi, 128, epi)

        flush()
        lam_init = 0.8 - 0.6 * math.exp(-0.3 * l)
        dma("sp", dl, diff_lambda[l].partition_broadcast(128), "m_dl", [], ["dl"])
        dma("sp", small[:, 16:17], diff_subln[l].rearrange("(p o) -> p o", o=1), "m_sub", [], ["small2"], slow=True)
        tt("dve", att_f[0][:, 0:64], dl[:, 0:64], dl[:, 64:128], ALU.mult, ["dl"], [("af", 0)])
        tt("dve", att_f[0][:, 64:128], dl[:, 128:192], dl[:, 192:256], ALU.mult, ["dl", ("af", 0)], [("af", 0)])
        P.op("dve", lambda e: e.reduce_sum(small[:, 20:21], att_f[0][:, 0:64], mybir.AxisListType.X), [("af", 0)],
             ["lam"])
        P.op("dve", lambda e: e.reduce_sum(small[:, 21:22], att_f[0][:, 64:128], mybir.AxisListType.X), [("af", 0), "lam"],
             ["lam"])
        act(small[:, 22:24], small[:, 20:22], AF.Exp, ["lam"], ["lam"])
        stt("dve", small[:, 24:25], small[:, 23:24], float(-lam_init), small[:, 22:23], ALU.add, ALU.subtract,
            ["lam"], ["lam"])
        ts("dve", small[:, 25:26], small[:, 16:17], float(1.0 - lam_init), None, ALU.mult, None, ["small2", "lam"],
           ["lam"])
        neglam = small[:, 24:25]
        sgc = small[:, 25:26]
        eps5 = small[:, 26:27]
        P.op("dve", lambda e: e.memset(eps5, 1e-5), [], ["eps5"])
        SL = strip_len("B")
        sB = SV[:, 0:2 * SL].rearrange("p (s j) -> p s j", j=SL)
        for h in range(4):
            flush()
            load_q(QT[0], 1536 + h * 128, 128, "qt0", ("QT", 0))
            load_k(l, KTb[0], 512 + h * 128, 128, "kt0", ("KT", 0))
            load_v(l, Vb[:, :, 0:128], 512 + h * 128, 128)
            dma("sp", sB, stripB[h].rearrange("s p j -> p s j"), "strips", [], STRIP_R)
            for qg in range(NTG):
                for m in range(2):
                    items = []
                    for kb, off in band_tiles("B", qg):
                        items.append((KTb[0][m * 64:(m + 1) * 64, kb * 128:(kb + 1) * 128],
                                      QT[0][m * 64:(m + 1) * 64, qg * 512:(qg + 1) * 512],
                                      Vb[:, kb, 0:128], khalf(kb) * SL + off,
                                      KTR(0) + [("QT", 0)] + VR, None))
                    def epi(m=m, h=h, qg=qg):
                        act(att_f[m], ps[6 + m], AF.Ln, [("ps", 6 + m)], [("af", m)])
                        act(att_f[m], att_f[m], AF.Exp, [("af", m)], [("af", m)], scale=-1.0)
                        tt("dve", att_f[2 + m], ps[4 + m], att_f[m], ALU.mult, [("ps", 4 + m), ("af", m)],
                           [("af", 2 + m)])
                        if m == 0:
                            return
                        stt("dve", att_f[4], att_f[3], neglam, att_f[2], ALU.mult, ALU.add,
                            [("af", 2), ("af", 3), "lam"], [("af", 4)])
                        act(att_f[5], att_f[4], AF.Square, [("af", 4)], [("af", 5)])
                        sb = acnt["s"] % 4
                        acnt["s"] += 1
                        mm(ps[sb], ones_f, att_f[5], True, True, [("af", 5), "ones_f"], [("ps", sb)])
                        act(att_f[0], ps[sb], AF.Ln, [("ps", sb)], [("af", 0)], bias=eps5[:, 0:1], scale=1.0 / 128.0)
                        act(att_f[0], att_f[0], AF.Exp, [("af", 0)], [("af", 0)], scale=-0.5)
                        oi = acnt["o"] % 2
                        acnt["o"] += 1
                        stt("dve", ob[oi], att_f[4], sgc, att_f[0], ALU.mult, ALU.mult,
                            [("af", 4), ("af", 0), "lam"], [("ob", oi)])
                        store_branch(ob[oi], 512 + h * 128, 128, qg, ("ob", oi))
                    attn_group(items, 4 + m, 6 + m, 128, epi)

        flush()
        icv = view(o_S + 8192, 16384)
        rl = att_f[0]
        hxs = view(o_V, 8192, BF16)
        dma("sp", icv[0:32, :], na_ic, "m_ic", [], STRIP_R)
        for h in range(8):
            dma("sp", rl[0:32, 0:23], rpbT[l, h], "m_rl", [], [("af", 0)])
            for cc in range(8):
                bank = cc % 3
                mm(ps[bank][0:23, :], rl[0:32, 0:23], icv[0:32, cc * 512:(cc + 1) * 512], True, True,
                   [("af", 0), "strips"], [("ps", bank)])
                act(hxs[0:23, cc * 512:(cc + 1) * 512], ps[bank][0:23, :], AF.Exp, [("ps", bank)], VR)
            dma("sp", HX[h].rearrange("a k q -> a (k q)"), hxs[0:23, :], "hxst", VR, [("HX", h)])
        P.barrier()
        T1 = SV[:, 0:8 * 512].rearrange("p (j c) -> p j c", c=512)

        def mt_ap(base):
            b0 = mt_sb[0:2, base:base + 8]
            return bass.AP(b0.tensor, b0.offset, [list(b0.ap[0]), [1, 8], [0, 64]])

        for h in range(8):
            flush()
            load_q(QT[0], h * 64, 64, "qt0", ("QT", 0))
            load_k_rel(l, KTb[0], h * 64, 64, "kt0", ("KT", 0))
            load_v_rel(l, Vb[:, :, 0:64], h * 64, 64)
            for j in range(8):
                for kr2 in range(2):
                    a0 = 15 - 2 * j - kr2
                    k_ = j * 2 + kr2
                    dma("sp", T1[kr2 * 64:(kr2 + 1) * 64, j, :].rearrange("p (a q) -> p a q", q=64),
                        HX[h, a0:a0 + 8, :, :].rearrange("a k q -> k a q"), "strips", [],
                        ["strips" if k_ == 15 else ("T1", k_)])
            for g in range(NTG):
                items = []
                for j in range(8):
                    kb = 4 * g - 2 + j + 8
                    items.append((KTb[0][0:64, kb * 128:(kb + 1) * 128], QT[0][0:64, g * 512:(g + 1) * 512],
                                  Vb[:, kb, 0:64], j * 512,
                                  KTR(0) + [("QT", 0)] + VR, mt_ap((g * 8 + j) * 8)))
                oi = acnt["o"] % 2
                acnt["o"] += 1

                def epi(oi=oi, h=h, g=g):
                    pO, pZ = 4 + oi, 6 + oi
                    zt = att_f[1 + oi]
                    act(zt[0:64, :], ps[pZ][0:64, :], AF.Ln, [("ps", pZ)], [("af", 1 + oi)])
                    act(zt[0:64, :], zt[0:64, :], AF.Exp, [("af", 1 + oi)], [("af", 1 + oi)], scale=-1.0)
                    tt("dve", ob[oi][0:64, :], ps[pO][0:64, :], zt[0:64, :], ALU.mult, [("ps", pO), ("af", 1 + oi)],
                       [("ob", oi)])
                    store_branch(ob[oi][0:64, :], h * 64, 64, g, ("ob", oi))
                attn_group(items, 4 + oi, 6 + oi, 64, epi)

    for tg in range(NTG):
        load_x_input(tg)
        if DEBUG["stop"] != 0:
            ffn(0, w_ffn1_in, w_ffn1_out, 0)
            qkv(0, tg)
        store_x_scratch(tg)
    for l in range(L):
        if DEBUG["stop"] in (0, 1):
            break
        exchange(l)
        attention(l)
        flush()
        P.barrier()
        if DEBUG["stop"] == 2:
            break
        for tg in range(NTG):
            load_x_scratch(tg)
            post(l, tg)
            ffn(l, w_ffn2_in, w_ffn2_out, l * 3 + 2)
            if l + 1 < L:
                ffn(l + 1, w_ffn1_in, w_ffn1_out, (l + 1) * 3)
                qkv(l + 1, tg)
                store_x_scratch(tg)
            else:
                final_out(tg)
    P.barrier()

    sems = {e: nc.alloc_semaphore("s_" + e) for e in COMPUTE}
    dma_sems = {k: nc.alloc_semaphore("d_" + k) for k in P.dma_cnt}
    with nc.Block() as block:
        P.replay(nc, block, sems, dma_sems)
    return nc


_CACHE = {}


def _run(cfg, xs, core_kinds, weights):
    key = (cfg.D, cfg.DFF, cfg.DEPTH, cfg.NTOK, cfg.ncores)
    if key not in _CACHE:
        _CACHE[key] = build_program(cfg)
    nc = _CACHE[key]
    L = cfg.DEPTH
    common = {
        "norm_ffn1": weights["norm_ffn1"], "w_ffn1_in": weights["w_ffn1_in"], "w_ffn1_out": weights["w_ffn1_out"],
        "norm_mix": weights["norm_mix"], "w_in": weights["w_in"], "rpbT": rpb_layout(np.asarray(weights["na_rpb"])),
        "diff_lambda": np.asarray(weights["diff_lambda"]).reshape(L, 256), "diff_subln": weights["diff_subln"],
        "swa_sink": weights["swa_sink"], "w_branch": weights["w_branch"], "w_gate": weights["w_gate"],
        "b_gate": weights["b_gate"], "w_out": weights["w_out"], "norm_ffn2": weights["norm_ffn2"],
        "w_ffn2_in": weights["w_ffn2_in"], "w_ffn2_out": weights["w_ffn2_out"],
        "norm_final": np.asarray(weights["norm_final"]).reshape(1, -1),
    }
    common = {k: np.ascontiguousarray(np.asarray(v, dtype=np.float32)) for k, v in common.items()}
    tabs = {}
    in_maps = []
    for x, kind in zip(xs, core_kinds):
        if kind not in tabs:
            tabs[kind] = host_tables(*kind)
        t = tabs[kind]
        m = dict(common)
        m["x"] = np.ascontiguousarray(x, dtype=np.float32)
        m["stripD"] = t["stripD"].astype(ml_dtypes.bfloat16)
        for g in range(3):
            m["stripC%d" % g] = t["stripC%d" % g].astype(ml_dtypes.bfloat16)
        m["stripB"] = t["stripB"].astype(ml_dtypes.bfloat16)
        m["na_ic"] = t["na_ic"]
        m["na_mt"] = t["na_mt"]
        m["na_kr"] = t["na_kr"]
        in_maps.append(m)
    res = run_bass_kernel_spmd(nc, in_maps, core_ids=list(range(len(in_maps))))
    if DEBUG["on"]:
        DEBUG["res"] = res.results
    return [r["y"] for r in res.results]


def kernel(x_prompt, x_sample, **weights):
    x_prompt = np.asarray(x_prompt, dtype=np.float32)
    x_sample = np.asarray(x_sample, dtype=np.float32)
    D = x_prompt.shape[-1]
    DFF = np.asarray(weights["w_ffn1_out"]).shape[1]
    L = np.asarray(weights["w_out"]).shape[0]
    nb = x_prompt.shape[0]
    ns = x_sample.shape[0]
    assert nb % 2 == 0 and x_prompt.shape[1] == 2048 and x_sample.shape[1] == 4096
    cfg = Cfg(D=D, DFF=DFF, DEPTH=L, NTOK=2048)
    cfg.ncores = nb + 2 * ns
    xs = []
    kinds = []
    for i in range(nb):
        xs.append(x_prompt[i])
        kinds.append((True, i % 2))
    for i in range(ns):
        for r in range(2):
            xs.append(x_sample[i, r * 2048:(r + 1) * 2048])
            kinds.append((False, r))
    ys = _run(cfg, xs, kinds, weights)
    y_prompt = np.stack(ys[:nb], 0)
    y_sample = np.stack([np.concatenate([ys[nb + 2 * i], ys[nb + 2 * i + 1]], 0) for i in range(ns)], 0)
    return (y_prompt.astype(np.float32), y_sample.astype(np.float32))
```

```python
import math
import numpy as np
import ml_dtypes
import concourse.bass as bass
import concourse.mybir as mybir
from concourse.bass_utils import run_bass_kernel_spmd

F32 = mybir.dt.float32
BF16 = mybir.dt.bfloat16
AF = mybir.ActivationFunctionType
ALU = mybir.AluOpType

NEG_BIG = -30000.0
DEBUG = {"on": False, "stop": None, "res": None}


class Cfg:
    def __init__(self, D=2048, DFF=5504, DEPTH=2, NTOK=2048):
        self.D = D
        self.DFF = DFF
        self.DEPTH = DEPTH
        self.NTOK = NTOK
        self.NK = 2 * NTOK
        self.KC = D // 128
        self.FC = DFF // 128
        self.TG = 512
        self.NTG = NTOK // 512
        self.NKB = self.NK // 128
        self.INW = 8448
        self.ncores = 8


class Op:
    __slots__ = ("eng", "fn", "deps", "needed", "ms", "dma_key", "dma_val")


COMPUTE = ("pe", "act", "dve", "pool", "sp")


class Prog:
    def __init__(self, same_engine_sync=True):
        self.ops = {e: [] for e in COMPUTE}
        self.last_w = {}
        self.readers = {}
        self.dma_cnt = {}
        self.last_dma = {}
        self.dma_inc = {}
        self.pending = {e: [] for e in COMPUTE}
        self.same_engine_sync = same_engine_sync

    @staticmethod
    def _stream(op):
        return ("d", op.dma_key) if op.dma_key is not None else ("e", op.eng)

    def op(self, eng, fn, reads=(), writes=(), dma_key=None, inc=16):
        o = Op()
        o.eng = eng
        o.fn = fn
        o.needed = False
        o.ms = None
        o.dma_key = dma_key
        o.dma_val = None
        deps = {}

        def add(d, raw=True):
            if d is None:
                return
            if d.dma_key is None and d.eng == eng and dma_key is None:
                if eng == "pe" or not self.same_engine_sync or not raw:
                    return
            s = self._stream(d)
            cur = deps.get(s)
            if cur is None or self._order(d) > self._order(cur):
                deps[s] = d

        for r in reads:
            add(self.last_w.get(r))
        for w in writes:
            add(self.last_w.get(w), raw=False)
            rd = self.readers.get(w)
            if rd:
                for d in rd.values():
                    add(d, raw=False)
        for d in self.pending[eng]:
            add(d)
        self.pending[eng] = []
        o.deps = list(deps.values())
        for d in o.deps:
            d.needed = True
        if dma_key is not None:
            c = self.dma_cnt.get(dma_key, 0) + 1
            self.dma_cnt[dma_key] = c
            o.dma_val = inc * c
            self.dma_inc[dma_key] = inc
            self.last_dma[dma_key] = o
        o.ms = len(self.ops[eng])
        self.ops[eng].append(o)
        for r in reads:
            self.readers.setdefault(r, {})[self._stream(o)] = o
        for w in writes:
            self.last_w[w] = o
            self.readers[w] = {}
        return o

    @staticmethod
    def _order(o):
        return o.dma_val if o.dma_key is not None else o.ms

    def barrier(self):
        toks = []
        for e in COMPUTE:
            for o in reversed(self.ops[e]):
                if o.dma_key is None:
                    toks.append(o)
                    break
        for k, o in self.last_dma.items():
            toks.append(o)
        for e in COMPUTE:
            self.pending[e] = list(toks)

    def replay(self, nc, block, sems, dma_sems):
        for e in COMPUTE:
            c = 0
            for o in self.ops[e]:
                if o.dma_key is None:
                    if o.needed:
                        c += 1
                        o.ms = c
                    else:
                        o.ms = None

        def run(engname):
            def body(e):
                water = {}
                for o in self.ops[engname]:
                    for d in o.deps:
                        if d.dma_key is not None:
                            s, v = dma_sems[d.dma_key], d.dma_val
                        else:
                            s, v = sems[d.eng], d.ms
                        key = id(s)
                        if water.get(key, 0) < v:
                            e.wait_ge(s, v)
                            water[key] = v
                    ins = o.fn(e)
                    if o.dma_key is not None:
                        ins.then_inc(dma_sems[o.dma_key], self.dma_inc[o.dma_key])
                    elif o.needed:
                        ins.then_inc(sems[engname], 1)
                if engname == "sp":
                    for k, c in self.dma_cnt.items():
                        e.wait_ge(dma_sems[k], self.dma_inc[k] * c)
            return body

        block.tensor(run("pe"))
        block.scalar(run("act"))
        block.vector(run("dve"))
        block.gpsimd(run("pool"))
        block.sync(run("sp"))


def alibi_slopes(n):
    return np.exp2(-8.0 * np.arange(1, n + 1, dtype=np.float64) / n)


BANDS = {
    "D": (-1, 4), "C0": (-1, 4), "C1": (-2, 5), "C2": (-8, 11), "B": (-28, 31),
}
RMIN, RMAX = -12, 31


def urange(name):
    lo, hi = BANDS[name]
    return max(lo, RMIN), min(16 + hi, RMAX)


def strip_len(name):
    ulo, uhi = urange(name)
    return 512 + (uhi - ulo) * 128


def band_len(name):
    lo, hi = BANDS[name]
    return 512 + (hi - lo) * 128


def rel_ok(name, rel):
    lo, hi = BANDS[name]
    ulo, uhi = urange(name)
    if rel < ulo or rel > uhi:
        return False
    return (lo <= rel <= hi) or (16 + lo <= rel <= 16 + hi)


def make_strip(slope, uhi, length, rank, radius=None, dil=1):
    p = np.arange(128)[:, None]
    j = np.arange(length)[None, :]
    d = uhi * 128 + p - j - 2048 * rank
    ad = np.abs(d).astype(np.float64)
    v = np.exp(-slope * ad)
    if radius is not None:
        v = np.where(ad <= radius, v, 0.0)
    if dil > 1:
        v = np.where((d % dil) == 0, v, 0.0)
    return v.astype(np.float32)


def host_tables(pair_core, rank):
    sl = alibi_slopes(24)
    t = {}

    def halves(strip):
        out = np.stack([strip, strip], 0)
        if pair_core:
            out[1 - rank] = 0.0
        return out

    def regions(strip):
        out = np.stack([strip, strip, strip], 0)
        if pair_core or rank == 0:
            out[0] = 0.0
        if pair_core or rank == 1:
            out[2] = 0.0
        return out

    lo, hi = BANDS["D"]
    t["stripD"] = np.stack([regions(make_strip(sl[h], hi, band_len("D"), 0, radius=128)) for h in range(8)], 0)
    for gi, (nm, dil) in enumerate((("C0", 1), ("C1", 4), ("C2", 16))):
        lo, hi = BANDS[nm]
        t["strip" + nm] = np.stack(
            [regions(make_strip(sl[8 + gi * 4 + h], hi, band_len(nm), 0, radius=64 * dil, dil=dil))
             for h in range(4)], 0)
    ulo, uhi = urange("B")
    t["stripB"] = np.stack([halves(make_strip(sl[20 + h], uhi, strip_len("B"), rank)) for h in range(4)], 0)
    kc = np.arange(64)[:, None]
    qc = np.arange(64)[None, :]
    dc = np.clip(kc - qc + 15, 0, 30)
    ic = np.zeros((32, 64, 64), np.float32)
    for b in range(31):
        ic[b] = (dc == b)
    c0 = np.clip(qc - 8, 0, 48)
    colok = (kc >= c0) & (kc < c0 + 16)
    ic[31] = np.where(colok, 0.0, NEG_BIG)
    t["na_ic"] = ic.reshape(32, 4096)
    mt = np.full((2, 4, 8, 8), NEG_BIG, np.float32)
    seq_rows = 32 if pair_core else 64
    for g in range(4):
        for j in range(8):
            rb = 4 * g - 2 + j
            if rb < 0 and (pair_core or rank == 0):
                continue
            if rb >= 16 and (pair_core or rank == 1):
                continue
            for qr8 in range(8):
                qr = 8 * (g + 4 * rank) + qr8
                s0 = (qr // seq_rows) * seq_rows
                r0 = s0 + min(max(qr - s0 - 4, 0), seq_rows - 8)
                for kr2 in range(2):
                    kr = 2 * (16 * rank + rb) + kr2
                    if r0 <= kr < r0 + 8:
                        mt[kr2, g, j, qr8] = 0.0
    t["na_mt"] = mt.reshape(2, 256)
    kr = np.zeros((2, 128), np.float32)
    kr[0, 0:64] = 1.0
    kr[1, 64:128] = 1.0
    t["na_kr"] = kr
    return t


def rpb_layout(na_rpb):
    L = na_rpb.shape[0]
    out = np.zeros((L, 8, 32, 23), np.float32)
    for a in range(15):
        out[:, :, 0:31, a + 4] = na_rpb[:, :, 14 - a, :]
    out[:, :, 31, :] = 1.0
    return out


def build_program(cfg):
    D, DFF, L, NTOK, KC, FC = cfg.D, cfg.DFF, cfg.DEPTH, cfg.NTOK, cfg.KC, cfg.FC
    NTG, NKB, INW = cfg.NTG, cfg.NKB, cfg.INW
    nc = bass.Bass("TRN2", target_bir_lowering=False)
    P = Prog()

    def din(name, shape, dt=F32):
        return nc.dram_tensor(name, list(shape), dt, kind="ExternalInput").ap()

    x_in = din("x", [NTOK, D])
    y_out = nc.dram_tensor("y", [NTOK, D], F32, kind="ExternalOutput").ap()
    norm_ffn1 = din("norm_ffn1", [L, D])
    w_ffn1_in = din("w_ffn1_in", [L, D, 2 * DFF])
    w_ffn1_out = din("w_ffn1_out", [L, DFF, D])
    norm_mix = din("norm_mix", [L, D])
    w_in = din("w_in", [L, D, INW])
    rpbT = din("rpbT", [L, 8, 32, 23])
    diff_lambda = din("diff_lambda", [L, 256])
    diff_subln = din("diff_subln", [L, 128])
    swa_sink = din("swa_sink", [L, 8])
    w_branch = din("w_branch", [L, 4, 512, D])
    w_gate = din("w_gate", [L, D, 4 * D])
    b_gate = din("b_gate", [L, 4 * D])
    w_out = din("w_out", [L, D, D])
    norm_ffn2 = din("norm_ffn2", [L, D])
    w_ffn2_in = din("w_ffn2_in", [L, D, 2 * DFF])
    w_ffn2_out = din("w_ffn2_out", [L, DFF, D])
    norm_final = din("norm_final", [1, D])
    stripD = din("stripD", [8, 3, 128, band_len("D")], BF16)
    stripC = [din("stripC%d" % g, [4, 3, 128, band_len("C%d" % g)], BF16) for g in range(3)]
    stripB = din("stripB", [4, 2, 128, strip_len("B")], BF16)
    na_ic = din("na_ic", [32, 4096])
    na_mt = din("na_mt", [2, 256])
    na_kr = din("na_kr", [2, 128])

    skind = "ExternalOutput" if DEBUG["on"] else "Internal"
    XT = nc.dram_tensor("XT", [KC, 128, NTOK], F32, kind=skind).ap()
    QKT = nc.dram_tensor("QKT", [INW, NTOK], BF16, kind=skind).ap()
    NKV = 2688
    NK = cfg.NK
    KVP = [(2560, 128), (1024, 512), (1536, 512), (2048, 512), (512, 512), (0, 512)]
    KTo = {o: nc.dram_tensor("KTo%d" % o, [w, NTOK], BF16).ap() for o, w in KVP}
    VTo = {o: nc.dram_tensor("VTo%d" % o, [NTOK, w], BF16).ap() for o, w in KVP}
    KTa = [{o: nc.dram_tensor("KTa%d_%d" % (l, o), [2 * w, NTOK], BF16).ap() for o, w in KVP} for l in range(L)]
    VTa = [{o: nc.dram_tensor("VTa%d_%d" % (l, o), [NK, w], BF16).ap() for o, w in KVP} for l in range(L)]

    def panel_of(off):
        po = max(o for o, w in KVP if o <= off)
        return po, off - po
    GROUPS = [[2 * i, 2 * i + 1] for i in range(cfg.ncores // 2)]
    BRT = nc.dram_tensor("BRT", [2048, NTOK], BF16, kind=skind).ap()
    HX = nc.dram_tensor("HX", [8, 23, 64, 64], BF16).ap()

    TOTAL = 206 * 1024
    arena = nc.alloc_sbuf_tensor("arena", [128, TOTAL // 4], F32)
    cursor = [0]

    def carve(nbytes):
        off = cursor[0]
        cursor[0] += (nbytes + 63) // 64 * 64
        assert cursor[0] <= TOTAL, ("SBUF overflow", cursor[0])
        return off

    def view(off, nbytes, dt=F32, pat=None, **kw):
        a = arena[:, off // 4:(off + nbytes) // 4]
        if dt != F32:
            a = a.bitcast(dt)
        if pat is not None:
            a = a.rearrange(pat, **kw)
        return a

    o_ident = carve(512)
    ident = view(o_ident, 512)
    o_onesf = carve(512)
    ones_f = view(o_onesf, 512)
    o_onesb = carve(256)
    ones_b = view(o_onesb, 256, BF16)
    NG = 3 * L + 1
    o_gain = carve(NG * KC * 4)
    gains = view(o_gain, NG * KC * 4)
    o_bg = carve(L * 4 * KC * 4)
    bgate = view(o_bg, L * 4 * KC * 4)
    o_small = carve(64 * 4)
    small = view(o_small, 64 * 4)
    o_dl = carve(256 * 4)
    dl = view(o_dl, 256 * 4)
    o_mt = carve(512)
    mt_sb = view(o_mt, 512, BF16)
    o_kr = carve(256)
    kr_sb = view(o_kr, 256, BF16)
    NW = 4
    o_w = [carve(16 * 512 * 2) for _ in range(NW)]
    wslots = [view(o, 16 * 512 * 2, BF16, "p (k n) -> p k n", n=512) for o in o_w]
    base = cursor[0]

    cursor[0] = base
    o_xT = carve(KC * 512 * 4)
    xT = view(o_xT, KC * 512 * 4, F32, "p (c t) -> p c t", t=512)
    o_hT = carve(KC * 512 * 2)
    hT = view(o_hT, KC * 512 * 2, BF16, "p (c t) -> p c t", t=512)
    NAR = max(FC, 32)
    o_ar = carve(NAR * 512 * 2)
    ar = view(o_ar, NAR * 512 * 2, BF16, "p (c t) -> p c t", t=512)
    o_macc = carve(4 * 512 * 4)
    macc = view(o_macc, 4 * 512 * 4, F32, "p (c t) -> p c t", t=512)
    tmpf = []
    for i in range(4):
        o = carve(2048)
        tmpf.append(view(o, 2048))
    stg = []
    for i in range(4):
        o = carve(1024)
        stg.append(view(o, 1024, BF16))
    o_rstd = carve(2048)
    rstd = view(o_rstd, 2048)
    xstage = [view(o_ar + i * D * 4, D * 4) for i in range(2)]
    assert 2 * D * 4 <= NAR * 512 * 2
    tok_end = cursor[0]

    cursor[0] = base
    QT = []
    KTb = []
    for i in range(3):
        o = carve(NTOK * 2)
        QT.append(view(o, NTOK * 2, BF16))
    for i in range(3):
        o = carve(NK * 2)
        KTb.append(view(o, NK * 2, BF16))
    o_V = carve(NKB * 384 * 2)
    Vb = view(o_V, NKB * 384 * 2, BF16, "p (k c) -> p k c", c=384)
    SBYTES = max(2 * strip_len("B") * 2, 2 * 3 * sum(band_len("C%d" % g) for g in range(3)))
    o_S = carve(SBYTES)
    NE, NP_, LA = 2, 4, 2
    Eb = []
    for i in range(NE):
        o = carve(4096)
        Eb.append(view(o, 4096))
    Pb = []
    for i in range(NP_):
        o = carve(2048)
        Pb.append(view(o, 2048, BF16))

    att_f = []
    for i in range(6):
        o = carve(2048)
        att_f.append(view(o, 2048))
    ob = []
    for i in range(2):
        o = carve(1024)
        ob.append(view(o, 1024, BF16))
    att_end = cursor[0]
    assert max(tok_end, att_end) <= TOTAL

    pd = [nc.alloc_psum_tensor("pd%d" % i, [128, 1024], F32).ap() for i in range(4)]
    ps = [pd[i // 2][:, (i % 2) * 512:(i % 2 + 1) * 512] for i in range(8)]

    def mm(out, lhsT, rhs, start, stop, reads, writes):
        return P.op("pe", lambda e: e.matmul(out, lhsT=lhsT, rhs=rhs, start=start, stop=stop), reads, writes)

    def act(out, in_, func, reads, writes, bias=None, scale=None):
        kw = {}
        if bias is not None:
            kw["bias"] = bias
        if scale is not None:
            kw["scale"] = scale
        return P.op("act", lambda e: e.activation(out, in_, func, **kw), reads, writes)

    def tt(eng, out, in0, in1, op, reads, writes):
        return P.op(eng, lambda e: e.tensor_tensor(out, in0, in1, op), reads, writes)

    def stt(eng, out, in0, scalar, in1, op0, op1, reads, writes):
        return P.op(eng, lambda e: e.scalar_tensor_tensor(out, in0, scalar, in1, op0, op1), reads, writes)

    def ts(eng, out, in0, s1, s2, op0, op1, reads, writes):
        if s2 is None:
            return P.op(eng, lambda e: e.tensor_scalar(out, in0, s1, None, op0), reads, writes)
        return P.op(eng, lambda e: e.tensor_scalar(out, in0, s1, s2, op0, op1), reads, writes)

    def recip(out, in_, reads, writes):
        return P.op("dve", lambda e: e.reciprocal(out, in_), reads, writes)

    def copy(eng, out, in_, reads, writes):
        return P.op(eng, lambda e: e.tensor_copy(out, in_), reads, writes)

    def dma(eng, out, in_, key, reads, writes, slow=False):
        if slow:
            return P.op(eng, lambda e: e.dma_start(out=out, in_=in_, allow_slow_non_contiguous=True),
                        reads, writes, dma_key=key)
        return P.op(eng, lambda e: e.dma_start(out=out, in_=in_), reads, writes, dma_key=key)

    wcnt = [0]

    def wload(src, nk, ncol):
        s = wcnt[0] % NW
        wcnt[0] += 1
        dma("pool", wslots[s][:, 0:nk, 0:ncol], src, "w%d" % s, [], [("w", s)])
        return wslots[s], ("w", s)

    def wview(w2d):
        return w2d.rearrange("(kc p) n -> p kc n", p=128)

    P.op("pool", lambda e: e.memset(ident, 0.0), [], ["ident"])
    P.op("pool", lambda e: e.affine_select(out=ident, in_=ident, pattern=[[-1, 128]], compare_op=ALU.not_equal,
                                           fill=1.0, base=0, channel_multiplier=1), ["ident"], ["ident"])
    P.op("pool", lambda e: e.memset(ones_f, 1.0), [], ["ones_f"])
    P.op("pool", lambda e: e.memset(ones_b, 1.0), [], ["ones_b"])
    gsrc = [norm_ffn1, norm_mix, norm_ffn2]
    for l in range(L):
        for i in range(3):
            gi = l * 3 + i
            dma("sp", gains[:, gi * KC:(gi + 1) * KC], gsrc[i][l].rearrange("(c p) -> p c", p=128), "setup",
                [], [("gains", gi)], slow=True)
        dma("sp", bgate[:, l * 4 * KC:(l + 1) * 4 * KC], b_gate[l].rearrange("(c p) -> p c", p=128), "setup",
            [], [("bgate", l)], slow=True)
    dma("sp", gains[:, 3 * L * KC:(3 * L + 1) * KC], norm_final[0].rearrange("(c p) -> p c", p=128), "setup",
        [], [("gains", 3 * L)], slow=True)
    zero_rows_early = lambda ap, r: P.op("pool", lambda e: e.memset(ap, 0.0), [], r)
    zero_rows_early(mt_sb, ["natt_c"])
    zero_rows_early(kr_sb, ["natt_c2"])
    dma("pool", mt_sb[0:2, :], na_mt, "setup2", [], ["natt_c"])
    dma("pool", kr_sb[0:2, :], na_kr, "setup2", [], ["natt_c2"])
    CONST_R = ["ident", "ones_f", "ones_b", "gains", "bgate", "rv"]
    P.barrier()

    def gain_col(gi, c):
        return gains[:, gi * KC + c:gi * KC + c + 1]

    def R_x(c):
        return ("xT", c)

    def R_h(c):
        return ("hT", c)

    def R_ar(c):
        return ("ar", c)

    def load_x_input(tg):
        for tbl in range(4):
            tb = tg * 4 + tbl
            st = xstage[tbl % 2]
            rs = ("xstage", tbl % 2)
            dma("sp", st, x_in[tb * 128:(tb + 1) * 128, :], "xs%d" % (tbl % 2), [], [rs] + [R_ar(i) for i in range(NAR)])
            for cg in range(KC // 4):
                bank = (tbl * (KC // 4) + cg) % 4
                for ci in range(4):
                    c = cg * 4 + ci
                    P.op("pe", (lambda o, i: (lambda e: e.transpose(o, i, ident)))(ps[bank][:, ci * 128:(ci + 1) * 128],
                                                                                 st[:, c * 128:(c + 1) * 128]),
                         [rs], [("ps", bank)])
                eng = "dve" if cg % 2 == 0 else "act"
                o_ap = xT[:, cg * 4:(cg + 1) * 4, tbl * 128:(tbl + 1) * 128]
                i_ap = ps[bank].rearrange("p (c t) -> p c t", t=128)
                if eng == "dve":
                    copy("dve", o_ap, i_ap, [("ps", bank)], [R_x(cg * 4 + i) for i in range(4)])
                else:
                    P.op("act", (lambda o, i: (lambda e: e.copy(o, i)))(o_ap, i_ap), [("ps", bank)],
                         [R_x(cg * 4 + i) for i in range(4)])

    def load_x_scratch(tg):
        dma("sp", xT, XT.rearrange("c p t -> p c t")[:, :, tg * 512:(tg + 1) * 512], "xl", [("XT", tg)],
            [R_x(c) for c in range(KC)])

    def store_x_scratch(tg):
        dma("sp", XT.rearrange("c p t -> p c t")[:, :, tg * 512:(tg + 1) * 512], xT, "xst",
            [R_x(c) for c in range(KC)], [("XT", tg)])

    def norm_stats(src_fn, nchunks, nrows, divisor, eps, bank):
        for c in range(nchunks):
            src, rres = src_fn(c)
            sq = tmpf[c % 2]
            act(sq[0:nrows, :], src, AF.Square, [rres], [("tmpf", c % 2)])
            mm(ps[bank], ones_f[0:nrows, :], sq[0:nrows, :], c == 0, c == nchunks - 1, [("tmpf", c % 2), "ones_f"],
               [("ps", bank)])
        act(tmpf[2], ps[bank], AF.Sqrt, [("ps", bank)], [("tmpf", 2)], bias=float(eps), scale=1.0 / divisor)
        recip(rstd, tmpf[2], [("tmpf", 2)], ["rstd"])

    def norm_h(gi):
        norm_stats(lambda c: (xT[:, c, :], R_x(c)), KC, 128, float(D), 1e-6, 7)
        for c in range(KC):
            stt("dve", hT[:, c, :], xT[:, c, :], gain_col(gi, c), rstd, ALU.mult, ALU.mult,
                [R_x(c), "rstd", "gains"], [R_h(c)])

    def ffn(l, w_inp, w_outp, gi):
        norm_h(gi)
        wv = wview(w_inp[l])
        npan = (FC + 3) // 4
        for fp in range(npan):
            nf = min(4, FC - 4 * fp)
            wg, rg = wload(wv[:, :, fp * 512:fp * 512 + nf * 128], KC, nf * 128)
            wu, ru = wload(wv[:, :, DFF + fp * 512:DFF + fp * 512 + nf * 128], KC, nf * 128)
            for j in range(nf):
                f = fp * 4 + j
                ba, bb = 2 * (f % 2), 2 * (f % 2) + 1
                for k in range(KC):
                    mm(ps[ba], wg[:, k, j * 128:(j + 1) * 128], hT[:, k, :], k == 0, k == KC - 1, [rg, R_h(k)],
                       [("ps", ba)])
                for k in range(KC):
                    mm(ps[bb], wu[:, k, j * 128:(j + 1) * 128], hT[:, k, :], k == 0, k == KC - 1, [ru, R_h(k)],
                       [("ps", bb)])
                t = tmpf[f % 2]
                act(t, ps[ba], AF.Silu, [("ps", ba)], [("tmpf", f % 2)])
                tt("dve", ar[:, f, :], t, ps[bb], ALU.mult, [("tmpf", f % 2), ("ps", bb)], [R_ar(f)])
        wo = wview(w_outp[l])
        nfb = (FC + 15) // 16
        for dgp in range(KC // 4):
            b0 = 4 * (dgp % 2)
            for fb in range(nfb):
                nfk = min(16, FC - 16 * fb)
                wt, rw = wload(wo[:, fb * 16:fb * 16 + nfk, dgp * 512:(dgp + 1) * 512], nfk, 512)
                for fi in range(nfk):
                    f = fb * 16 + fi
                    for j in range(4):
                        mm(ps[b0 + j], wt[:, fi, j * 128:(j + 1) * 128], ar[:, f, :], f == 0, f == FC - 1,
                           [rw, R_ar(f)], [("ps", b0 + j)])
            for j in range(4):
                c = dgp * 4 + j
                stt("dve", xT[:, c, :], ps[b0 + j], 0.5, xT[:, c, :], ALU.mult, ALU.add, [("ps", b0 + j), R_x(c)],
                    [R_x(c)])

    PANELS = [(0, 512, "q", 0.125, 0), (512, 512, "k", 1.0, 0), (1024, 512, "v", 1.0, 0),
              (1536, 512, "q", 0.125, 1536), (2048, 512, "k", 1.0, 512), (2560, 512, "v", 1.0, 512)]
    for g in range(3):
        PANELS.append((3072 + g * 512, 512, "q", 128 ** -0.5, 3072 + g * 512))
    for g in range(3):
        PANELS.append((4608 + g * 512, 512, "k", 1.0, 1024 + g * 512))
    for g in range(3):
        PANELS.append((6144 + g * 512, 512, "v", 1.0, 1024 + g * 512))
    PANELS += [(7680, 512, "q", 0.125, 7680), (8192, 128, "k", 1.0, 2560), (8320, 128, "v", 1.0, 2560)]
    scnt = [0]

    def qkv(l, tg):
        norm_h(l * 3 + 1)
        wv = wview(w_in[l])
        pcnt = 0
        for (c0, ncol, kind, scale, dst) in PANELS:
            wt, rw = wload(wv[:, :, c0:c0 + ncol], KC, ncol)
            if kind != "v":
                for j in range(ncol // 128):
                    bank = pcnt % 4
                    pcnt += 1
                    for k in range(KC):
                        mm(ps[bank], wt[:, k, j * 128:(j + 1) * 128], hT[:, k, :], k == 0, k == KC - 1,
                           [rw, R_h(k)], [("ps", bank)])
                    s = scnt[0] % 4
                    scnt[0] += 1
                    if s % 2 == 0:
                        act(stg[s], ps[bank], AF.Copy, [("ps", bank)], [("stg", s)], scale=float(scale))
                    else:
                        ts("dve", stg[s], ps[bank], float(scale), None, ALU.mult, None, [("ps", bank)], [("stg", s)])
                    if kind == "q":
                        dap = QKT[dst + j * 128:dst + (j + 1) * 128, tg * 512:(tg + 1) * 512]
                    else:
                        dap = KTo[dst][j * 128:(j + 1) * 128, tg * 512:(tg + 1) * 512]
                    dma("sp", dap, stg[s], "stg%d" % s, [("stg", s)], [])
            else:
                for tbl in range(4):
                    bank = pcnt % 4
                    pcnt += 1
                    for k in range(KC):
                        mm(ps[bank][:, 0:ncol], hT[:, k, tbl * 128:(tbl + 1) * 128], wt[:, k, 0:ncol], k == 0,
                           k == KC - 1, [rw, R_h(k)], [("ps", bank)])
                    s = scnt[0] % 4
                    scnt[0] += 1
                    if s % 2 == 0:
                        act(stg[s][:, 0:ncol], ps[bank][:, 0:ncol], AF.Copy, [("ps", bank)], [("stg", s)])
                    else:
                        copy("dve", stg[s][:, 0:ncol], ps[bank][:, 0:ncol], [("ps", bank)], [("stg", s)])
                    r0 = tg * 512 + tbl * 128
                    dma("sp", VTo[dst][r0:r0 + 128, 0:ncol], stg[s][:, 0:ncol], "stg%d" % s, [("stg", s)], [])

    def post(l, tg):
        norm_h(l * 3 + 1)
        brT = ar
        dma("sp", ar[:, 0:16, :], BRT.rearrange("(c p) t -> p c t", p=128)[:, :, tg * 512:(tg + 1) * 512], "brl", [],
            [R_ar(c) for c in range(16)])
        wg_v = wview(w_gate[l])
        for dgp in range(KC // 4):
            for n in range(4):
                wgt, rg = wload(wg_v[:, :, n * D + dgp * 512:n * D + (dgp + 1) * 512], KC, 512)
                wbr, rb = wload(wview(w_branch[l, n])[:, :, dgp * 512:(dgp + 1) * 512], 4, 512)
                for j in range(4):
                    dc = dgp * 4 + j
                    pg, pp = 2 * (j % 2), 2 * (j % 2) + 1
                    for k in range(KC):
                        mm(ps[pg], wgt[:, k, j * 128:(j + 1) * 128], hT[:, k, :], k == 0, k == KC - 1, [rg, R_h(k)],
                           [("ps", pg)])
                    for c in range(4):
                        mm(ps[pp], wbr[:, c, j * 128:(j + 1) * 128], brT[:, n * 4 + c, :], c == 0, c == 3,
                           [rb, R_ar(n * 4 + c)], [("ps", pp)])
                    sg = tmpf[j % 2]
                    bcol = bgate[:, l * 4 * KC + n * KC + dc:l * 4 * KC + n * KC + dc + 1]
                    act(sg, ps[pg], AF.Sigmoid, [("ps", pg), "bgate"], [("tmpf", j % 2)], bias=bcol)
                    if n == 0:
                        tt("dve", macc[:, j, :], sg, ps[pp], ALU.mult, [("tmpf", j % 2), ("ps", pp)], [("macc", j)])
                    else:
                        t2 = tmpf[2 + j % 2]
                        tt("dve", t2, sg, ps[pp], ALU.mult, [("tmpf", j % 2), ("ps", pp)], [("tmpf", 2 + j % 2)])
                        if n < 3:
                            tt("dve", macc[:, j, :], macc[:, j, :], t2, ALU.add, [("macc", j), ("tmpf", 2 + j % 2)],
                               [("macc", j)])
                        else:
                            tt("dve", ar[:, 16 + dc, :], macc[:, j, :], t2, ALU.add,
                               [("macc", j), ("tmpf", 2 + j % 2)], [R_ar(16 + dc)])
        wo_v = wview(w_out[l])
        for dgp in range(KC // 4):
            wt, rw = wload(wo_v[:, :, dgp * 512:(dgp + 1) * 512], KC, 512)
            for j in range(4):
                c = dgp * 4 + j
                bank = 4 + (c % 4)
                for k in range(KC):
                    mm(ps[bank], wt[:, k, j * 128:(j + 1) * 128], ar[:, 16 + k, :], k == 0, k == KC - 1,
                       [rw, R_ar(16 + k)], [("ps", bank)])
                tt("dve", xT[:, c, :], ps[bank], xT[:, c, :], ALU.add, [("ps", bank), R_x(c)], [R_x(c)])

    def final_out(tg):
        gi = 3 * L
        norm_stats(lambda c: (xT[:, c, :], R_x(c)), KC, 128, float(D), 1e-6, 7)
        for c in range(KC):
            stt("dve", xT[:, c, :], xT[:, c, :], gain_col(gi, c), rstd, ALU.mult, ALU.mult,
                [R_x(c), "rstd", "gains"], [R_x(c)])
        for tbl in range(4):
            st = xstage[tbl % 2]
            rs = ("xstage", tbl % 2)
            for cg in range(KC // 4):
                bank = (tbl * (KC // 4) + cg) % 4
                for ci in range(4):
                    c = cg * 4 + ci
                    P.op("pe", (lambda o, i: (lambda e: e.transpose(o, i, ident)))(ps[bank][:, ci * 128:(ci + 1) * 128],
                                                                                 xT[:, c, tbl * 128:(tbl + 1) * 128]),
                         [R_x(c)], [("ps", bank)])
                o_ap = st[:, cg * 512:(cg + 1) * 512]
                if cg % 2 == 0:
                    copy("dve", o_ap, ps[bank], [("ps", bank)], [rs] + [R_ar(i) for i in range(NAR)])
                else:
                    P.op("act", (lambda o, i: (lambda e: e.copy(o, i)))(o_ap, ps[bank]), [("ps", bank)],
                         [rs] + [R_ar(i) for i in range(NAR)])
            r0 = tg * 512 + tbl * 128
            dma("sp", y_out[r0:r0 + 128, :], st, "yo%d" % (tbl % 2), [rs] + [R_ar(i) for i in range(NAR)], [])

    acnt = {"s": 0, "e": 0, "p": 0, "o": 0}

    pend = []
    SVv = view(o_S, SBYTES, BF16)
    STRIP_R = ["strips"] + [("T1", k) for k in range(15)]

    def flush(keep=0):
        while len(pend) > keep:
            pend.pop(0)()

    def sv_ap(o1, o2=None):
        base = SVv[:, o1:o1 + 512]
        if o2 is None:
            return base
        return bass.AP(base.tensor, base.offset, [list(base.ap[0]), [o2 - o1, 2], [1, 512]])

    def attn_group(items, psO, psZ, dv, epilogue):
        n = len(items)
        idx = 0
        while idx < n:
            grp = [idx] if idx + 1 >= n else [idx, idx + 1]
            if len(grp) == 2 and items[grp[1]][3] < items[grp[0]][3]:
                grp = [grp[1], grp[0]]
            idx += len(grp)
            sp = acnt["s"] % 2
            acnt["s"] += 1
            banks = [2 * sp, 2 * sp + 1][:len(grp)]
            for t, ii in enumerate(grp):
                kt_ap, qt_ap, v_ap, svo, rds, extra = items[ii]
                if extra is None:
                    mm(ps[banks[t]], kt_ap, qt_ap, True, True, rds, [("ps", banks[t])])
                else:
                    mm(ps[banks[t]], kt_ap, qt_ap, True, False, rds, [("ps", banks[t])])
                    mm(ps[banks[t]], kr_sb, extra, False, True, ["natt_c", "natt_c2"], [("ps", banks[t])])
            eb = acnt["e"] % NE
            acnt["e"] += 1
            pb = acnt["p"] % NP_
            acnt["p"] += 1
            rps = [("ps", b) for b in banks]
            if len(grp) == 2:
                act(Eb[eb], pd[sp], AF.Exp, rps, [("E", eb)])
                tt("dve", Pb[pb].rearrange("p (t c) -> p t c", c=512), Eb[eb].rearrange("p (t c) -> p t c", c=512),
                   sv_ap(items[grp[0]][3], items[grp[1]][3]), ALU.mult, [("E", eb)] + STRIP_R, [("P", pb)])
            else:
                act(Eb[eb][:, 0:512], ps[banks[0]], AF.Exp, rps, [("E", eb)])
                tt("dve", Pb[pb][:, 0:512], Eb[eb][:, 0:512], sv_ap(items[grp[0]][3]), ALU.mult,
                   [("E", eb)] + STRIP_R, [("P", pb)])

            def pv(grp=grp, pb=pb):
                for t, ii in enumerate(grp):
                    v_ap, rds = items[ii][2], items[ii][4]
                    first, last = ii == min(grp) and min(grp) == 0, ii == max(grp) and max(grp) == n - 1
                    st = (t == 0) and (0 in grp)
                    sp_ = (t == len(grp) - 1) and ((n - 1) in grp)
                    pslice = Pb[pb][:, t * 512:(t + 1) * 512]
                    mm(ps[psO][0:dv, :], v_ap, pslice, st, sp_, [("P", pb)] + rds, [("ps", psO)])
                    mm(ps[psZ], ones_b, pslice, st, sp_, [("P", pb), "ones_b"], [("ps", psZ)])
                if (n - 1) in grp:
                    epilogue()
            pend.append(pv)
            flush(LA)

    def store_branch(src, row0, nrows, qg, res):
        dma("sp", BRT[row0:row0 + nrows, qg * 512:(qg + 1) * 512], src, "bro%d" % res[1], [res], [])

    def khalf(kb):
        return kb // 16

    def load_q(dst, row0, nrows, key, res):
        dma("sp", dst[0:nrows, :], QKT[row0:row0 + nrows, :], key, [], [res])

    def KTR(i):
        return [("KT", i), ("KT", i, "a"), ("KT", i, "b")]

    VR = ["V"] + [("V", g_, x_) for g_ in range(3) for x_ in "abc"]

    def load_k(l, dst, row0, nrows, key, res, p0=0):
        po, lr = panel_of(row0)
        src = KTa[l][po].rearrange("(r f) t -> f r t", r=2)[lr:lr + nrows, :, :]
        dma("sp", dst[p0:p0 + nrows, :].rearrange("p (r t) -> p r t", r=2), src, key, [("KTa", po)], KTR(res[1]))

    def zero_rows(ap, res_w):
        P.op("pool", lambda e: e.memset(ap, 0.0), [], res_w)

    def load_v(l, dst, col0, ncols):
        po, lc = panel_of(col0)
        dma("sp", dst, VTa[l][po][:, lc:lc + ncols].rearrange("(k p) c -> p k c", p=128), "vb", [("VTa", po)], VR)

    def load_k_rel(l, dst, row0, nrows, key, res):
        po, lr = panel_of(row0)
        w = KTo[po].shape[0]
        i = res[1]
        dma("sp", dst[0:nrows, 0:1024], KTa[l][po][lr:lr + nrows, 1024:2048], key, [("KTa", po)], [("KT", i, "a")])
        dma("sp", dst[0:nrows, 1024:3072], KTo[po][lr:lr + nrows, :], key, [], [("KT", i, "b")])
        dma("sp", dst[0:nrows, 3072:4096], KTa[l][po][w + lr:w + lr + nrows, 0:1024], key, [("KTa", po)],
            [("KT", i)])

    def load_v_rel(l, dst, col0, ncols, tag=0, last=True):
        po, lc = panel_of(col0)
        dma("sp", dst[:, 0:8, :], VTa[l][po][1024:2048, lc:lc + ncols].rearrange("(k p) c -> p k c", p=128), "vb",
            [("VTa", po)], [("V", tag, "a")])
        dma("sp", dst[:, 8:24, :], VTo[po][:, lc:lc + ncols].rearrange("(k p) c -> p k c", p=128), "vb", [],
            [("V", tag, "b")])
        dma("sp", dst[:, 24:32, :], VTa[l][po][2048:3072, lc:lc + ncols].rearrange("(k p) c -> p k c", p=128), "vb",
            [("VTa", po)], ["V" if last else ("V", tag, "c")])

    def cc_gather(src, dst, res):
        P.op("pool", lambda e: e.collective_compute("AllGather", ALU.bypass, replica_groups=GROUPS,
                                                    ins=[src.opt()], outs=[dst.opt()]),
             [], ["ccres", res], dma_key="cc", inc=1)

    def exchange(l):
        P.barrier()
        for o, w in KVP:
            cc_gather(KTo[o], KTa[l][o], ("KTa", o))
            cc_gather(VTo[o], VTa[l][o], ("VTa", o))

    def band_tiles_rel(name, qg):
        lo, hi = BANDS[name]
        out = []
        for rel in range(lo, hi + 1):
            rb = 4 * qg + rel
            assert -8 <= rb < 24
            out.append((rb + 8, 0 if rb < 0 else (1 if rb < 16 else 2), (hi - rel) * 128))
        return out

    def band_tiles(name, qg):
        ulo, uhi = urange(name)
        out = []
        for kb in range(NKB):
            rel = kb - 4 * qg
            if rel_ok(name, rel):
                out.append((kb, (uhi - rel) * 128))
        return out

    def attention(l):
        SV = view(o_S, SBYTES, BF16)

        SL = band_len("D")
        sD = SV[:, 0:3 * SL].rearrange("p (s j) -> p s j", j=SL)
        esink = small[:, 8:16]
        dma("sp", small[:, 0:8], swa_sink[l].partition_broadcast(128), "m_sink", [], ["small"])
        act(esink, small[:, 0:8], AF.Exp, ["small"], ["esink"])
        zero_rows(QT[0][64:128, :], [("QT", 0)])
        zero_rows(KTb[0][64:128, :], KTR(0))
        load_v_rel(l, Vb[:, :, 0:128], 2560, 128)
        for hk in range(2):
            flush()
            load_k_rel(l, KTb[0], 2560 + hk * 64, 64, "kt0", ("KT", 0))
            for gq in range(4):
                hd = hk * 4 + gq
                load_q(QT[0], 7680 + hd * 64, 64, "qt0", ("QT", 0))
                dma("sp", sD, stripD[hd].rearrange("s p j -> p s j"), "strips", [], STRIP_R)
                for qg in range(NTG):
                    items = []
                    for kb, reg, off in band_tiles_rel("D", qg):
                        items.append((KTb[0][:, kb * 128:(kb + 1) * 128], QT[0][:, qg * 512:(qg + 1) * 512],
                                      Vb[:, kb, 0:128], reg * SL + off,
                                      KTR(0) + [("QT", 0)] + VR, None))
                    oi = acnt["o"] % 2
                    acnt["o"] += 1

                    def epi(oi=oi, hd=hd, qg=qg, r0=hk * 64):
                        pO, pZ = 4 + oi, 6 + oi
                        zt = att_f[oi]
                        act(zt[r0:r0 + 64, :], ps[pZ][r0:r0 + 64, :], AF.Ln, [("ps", pZ), "esink"], [("af", oi)],
                            bias=esink[r0:r0 + 64, hd:hd + 1])
                        act(zt[r0:r0 + 64, :], zt[r0:r0 + 64, :], AF.Exp, [("af", oi)], [("af", oi)], scale=-1.0)
                        tt("dve", ob[oi][r0:r0 + 64, :], ps[pO][r0:r0 + 64, :], zt[r0:r0 + 64, :], ALU.mult,
                           [("ps", pO), ("af", oi)], [("ob", oi)])
                        store_branch(ob[oi][r0:r0 + 64, :], 1536 + hd * 64, 64, qg, ("ob", oi))
                    attn_group(items, 4 + oi, 6 + oi, 128, epi)

        offs = [0]
        for g in range(3):
            offs.append(offs[-1] + 3 * band_len("C%d" % g))
        sC = [SV[:, offs[g]:offs[g + 1]].rearrange("p (s j) -> p s j", j=band_len("C%d" % g)) for g in range(3)]
        for h in range(4):
            flush()
            for g in range(3):
                load_q(QT[g], 3072 + g * 512 + h * 128, 128, "qt%d" % g, ("QT", g))
                load_k_rel(l, KTb[g], 1024 + g * 512 + h * 128, 128, "kt%d" % g, ("KT", g))
                load_v_rel(l, Vb[:, :, g * 128:(g + 1) * 128], 1024 + g * 512 + h * 128, 128, tag=g, last=(g == 2))
                dma("sp", sC[g], stripC[g][h].rearrange("s p j -> p s j"), "strips", [],
                    ["strips" if g == 2 else ("T1", g)])
            for qg in range(NTG):
                items = []
                for g in range(3):
                    for kb, reg, off in band_tiles_rel("C%d" % g, qg):
                        items.append((KTb[g][:, kb * 128:(kb + 1) * 128], QT[g][:, qg * 512:(qg + 1) * 512],
                                      Vb[:, kb, g * 128:(g + 1) * 128],
                                      offs[g] + reg * band_len("C%d" % g) + off,
                                      KTR(g) + [("QT", g)] + VR, None))
                oi = acnt["o"] % 2
                acnt["o"] += 1

                def epi(oi=oi, h=h, qg=qg):
                    pO, pZ = 4 + oi, 6 + oi
                    zt = att_f[oi]
                    act(zt, ps[pZ], AF.Ln, [("ps", pZ)], [("af", oi)])
                    act(zt, zt, AF.Exp, [("af", oi)], [("af", oi)], scale=-1.0)
                    tt("dve", ob[oi], ps[pO], zt, ALU.mult, [("ps", pO), ("af", oi)], [("ob", oi)])
                    store_branch(ob[oi], 1024 + h * 128, 128, qg, ("ob", oi))
                attn_group(items, 4 + oi, 6 + oi, 128, epi)

        flush()
        lam_init = 0.8 - 0.6 * math.exp(-0.3 * l)
        dma("sp", dl, diff_lambda[l].partition_broadcast(128), "m_dl", [], ["dl"])
        dma("sp", small[:, 16:17], diff_subln[l].rearrange("(p o) -> p o", o=1), "m_sub", [], ["small2"], slow=True)
        tt("dve", att_f[0][:, 0:64], dl[:, 0:64], dl[:, 64:128], ALU.mult, ["dl"], [("af", 0)])
        tt("dve", att_f[0][:, 64:128], dl[:, 128:192], dl[:, 192:256], ALU.mult, ["dl", ("af", 0)], [("af", 0)])
        P.op("dve", lambda e: e.reduce_sum(small[:, 20:21], att_f[0][:, 0:64], mybir.AxisListType.X), [("af", 0)],
             ["lam"])
        P.op("dve", lambda e: e.reduce_sum(small[:, 21:22], att_f[0][:, 64:128], mybir.AxisListType.X), [("af", 0), "lam"],
             ["lam"])
        act(small[:, 22:24], small[:, 20:22], AF.Exp, ["lam"], ["lam"])
        stt("dve", small[:, 24:25], small[:, 23:24], float(-lam_init), small[:, 22:23], ALU.add, ALU.subtract,
            ["lam"], ["lam"])
        ts("dve", small[:, 25:26], small[:, 16:17], float(1.0 - lam_init), None, ALU.mult, None, ["small2", "lam"],
           ["lam"])
        neglam = small[:, 24:25]
        sgc = small[:, 25:26]
        eps5 = small[:, 26:27]
        P.op("dve", lambda e: e.memset(eps5, 1e-5), [], ["eps5"])
        SL = strip_len("B")
        sB = SV[:, 0:2 * SL].rearrange("p (s j) -> p s j", j=SL)
        flush()
        zero_rows(KTb[1][64:128, :], KTR(1))
        zero_rows(KTb[2][0:64, :], KTR(2))
        for h in range(4):
            flush()
            load_q(QT[0], 1536 + h * 128, 128, "qt0", ("QT", 0))
            load_k(l, KTb[1], 512 + h * 128, 64, "kt1", ("KT", 1))
            load_k(l, KTb[2], 512 + h * 128 + 64, 64, "kt2", ("KT", 2), p0=64)
            load_v(l, Vb[:, :, 0:128], 512 + h * 128, 128)
            dma("sp", sB, stripB[h].rearrange("s p j -> p s j"), "strips", [], STRIP_R)
            for qg in range(NTG):
                for m in range(2):
                    items = []
                    for kb, off in band_tiles("B", qg):
                        items.append((KTb[1 + m][:, kb * 128:(kb + 1) * 128],
                                      QT[0][:, qg * 512:(qg + 1) * 512],
                                      Vb[:, kb, 0:128], khalf(kb) * SL + off,
                                      KTR(1 + m) + [("QT", 0)] + VR, None))
                    def epi(m=m, h=h, qg=qg):
                        act(att_f[m], ps[6 + m], AF.Ln, [("ps", 6 + m)], [("af", m)])
                        act(att_f[m], att_f[m], AF.Exp, [("af", m)], [("af", m)], scale=-1.0)
                        tt("dve", att_f[2 + m], ps[4 + m], att_f[m], ALU.mult, [("ps", 4 + m), ("af", m)],
                           [("af", 2 + m)])
                        if m == 0:
                            return
                        stt("dve", att_f[4], att_f[3], neglam, att_f[2], ALU.mult, ALU.add,
                            [("af", 2), ("af", 3), "lam"], [("af", 4)])
                        act(att_f[5], att_f[4], AF.Square, [("af", 4)], [("af", 5)])
                        sb = acnt["s"] % 4
                        acnt["s"] += 1
                        mm(ps[sb], ones_f, att_f[5], True, True, [("af", 5), "ones_f"], [("ps", sb)])
                        act(att_f[0], ps[sb], AF.Ln, [("ps", sb)], [("af", 0)], bias=eps5[:, 0:1], scale=1.0 / 128.0)
                        act(att_f[0], att_f[0], AF.Exp, [("af", 0)], [("af", 0)], scale=-0.5)
                        oi = acnt["o"] % 2
                        acnt["o"] += 1
                        stt("dve", ob[oi], att_f[4], sgc, att_f[0], ALU.mult, ALU.mult,
                            [("af", 4), ("af", 0), "lam"], [("ob", oi)])
                        store_branch(ob[oi], 512 + h * 128, 128, qg, ("ob", oi))
                    attn_group(items, 4 + m, 6 + m, 128, epi)

        flush()
        icv = view(o_S + 8192, 16384)
        rl = att_f[0]
        hxs = view(o_V, 8192, BF16)
        dma("sp", icv[0:32, :], na_ic, "m_ic", [], STRIP_R)
        for h in range(8):
            dma("sp", rl[0:32, 0:23], rpbT[l, h], "m_rl", [], [("af", 0)])
            for cc in range(8):
                bank = cc % 3
                mm(ps[bank][0:23, :], rl[0:32, 0:23], icv[0:32, cc * 512:(cc + 1) * 512], True, True,
                   [("af", 0), "strips"], [("ps", bank)])
                act(hxs[0:23, cc * 512:(cc + 1) * 512], ps[bank][0:23, :], AF.Exp, [("ps", bank)], VR)
            dma("sp", HX[h].rearrange("a k q -> a (k q)"), hxs[0:23, :], "hxst", VR, [("HX", h)])
        P.barrier()
        T1 = SV[:, 0:8 * 512].rearrange("p (j c) -> p j c", c=512)

        def mt_ap(base):
            b0 = mt_sb[:, base:base + 8]
            return bass.AP(b0.tensor, b0.offset, [list(b0.ap[0]), [1, 8], [0, 64]])

        zero_rows(QT[0][64:128, :], [("QT", 0)])
        zero_rows(KTb[0][64:128, :], KTR(0))
        for h in range(8):
            flush()
            load_q(QT[0], h * 64, 64, "qt0", ("QT", 0))
            load_k_rel(l, KTb[0], h * 64, 64, "kt0", ("KT", 0))
            if h % 2 == 0:
                load_v_rel(l, Vb[:, :, 0:128], h * 64, 128)
            for j in range(8):
                for kr2 in range(2):
                    a0 = 15 - 2 * j - kr2
                    k_ = j * 2 + kr2
                    dma("sp", T1[kr2 * 64:(kr2 + 1) * 64, j, :].rearrange("p (a q) -> p a q", q=64),
                        HX[h, a0:a0 + 8, :, :].rearrange("a k q -> k a q"), "strips", [],
                        ["strips" if k_ == 15 else ("T1", k_)])
            for g in range(NTG):
                items = []
                for j in range(8):
                    kb = 4 * g - 2 + j + 8
                    items.append((KTb[0][:, kb * 128:(kb + 1) * 128], QT[0][:, g * 512:(g + 1) * 512],
                                  Vb[:, kb, 0:128], j * 512,
                                  KTR(0) + [("QT", 0)] + VR, mt_ap((g * 8 + j) * 8)))
                oi = acnt["o"] % 2
                acnt["o"] += 1

                def epi(oi=oi, h=h, g=g, r0=(h % 2) * 64):
                    pO, pZ = 4 + oi, 6 + oi
                    zt = att_f[1 + oi]
                    act(zt[r0:r0 + 64, :], ps[pZ][r0:r0 + 64, :], AF.Ln, [("ps", pZ)], [("af", 1 + oi)])
                    act(zt[r0:r0 + 64, :], zt[r0:r0 + 64, :], AF.Exp, [("af", 1 + oi)], [("af", 1 + oi)], scale=-1.0)
                    tt("dve", ob[oi][r0:r0 + 64, :], ps[pO][r0:r0 + 64, :], zt[r0:r0 + 64, :], ALU.mult,
                       [("ps", pO), ("af", 1 + oi)], [("ob", oi)])
                    store_branch(ob[oi][r0:r0 + 64, :], h * 64, 64, g, ("ob", oi))
                attn_group(items, 4 + oi, 6 + oi, 128, epi)

    for tg in range(NTG):
        load_x_input(tg)
        if DEBUG["stop"] != 0:
            ffn(0, w_ffn1_in, w_ffn1_out, 0)
            qkv(0, tg)
        store_x_scratch(tg)
    for l in range(L):
        if DEBUG["stop"] in (0, 1):
            break
        exchange(l)
        attention(l)
        flush()
        P.barrier()
        if DEBUG["stop"] == 2:
            break
        for tg in range(NTG):
            load_x_scratch(tg)
            post(l, tg)
            ffn(l, w_ffn2_in, w_ffn2_out, l * 3 + 2)
            if l + 1 < L:
                ffn(l + 1, w_ffn1_in, w_ffn1_out, (l + 1) * 3)
                qkv(l + 1, tg)
                store_x_scratch(tg)
            else:
                final_out(tg)
    P.barrier()

    sems = {e: nc.alloc_semaphore("s_" + e) for e in COMPUTE}
    dma_sems = {k: nc.alloc_semaphore("d_" + k) for k in P.dma_cnt}
    with nc.Block() as block:
        P.replay(nc, block, sems, dma_sems)
    return nc


_CACHE = {}


def _run(cfg, xs, core_kinds, weights):
    key = (cfg.D, cfg.DFF, cfg.DEPTH, cfg.NTOK, cfg.ncores)
    if key not in _CACHE:
        _CACHE[key] = build_program(cfg)
    nc = _CACHE[key]
    L = cfg.DEPTH
    common = {
        "norm_ffn1": weights["norm_ffn1"], "w_ffn1_in": weights["w_ffn1_in"], "w_ffn1_out": weights["w_ffn1_out"],
        "norm_mix": weights["norm_mix"], "w_in": weights["w_in"], "rpbT": rpb_layout(np.asarray(weights["na_rpb"])),
        "diff_lambda": np.asarray(weights["diff_lambda"]).reshape(L, 256), "diff_subln": weights["diff_subln"],
        "swa_sink": weights["swa_sink"], "w_branch": weights["w_branch"], "w_gate": weights["w_gate"],
        "b_gate": weights["b_gate"], "w_out": weights["w_out"], "norm_ffn2": weights["norm_ffn2"],
        "w_ffn2_in": weights["w_ffn2_in"], "w_ffn2_out": weights["w_ffn2_out"],
        "norm_final": np.asarray(weights["norm_final"]).reshape(1, -1),
    }
    common = {k: np.ascontiguousarray(np.asarray(v, dtype=np.float32)) for k, v in common.items()}
    tabs = {}
    in_maps = []
    for x, kind in zip(xs, core_kinds):
        if kind not in tabs:
            tabs[kind] = host_tables(*kind)
        t = tabs[kind]
        m = dict(common)
        m["x"] = np.ascontiguousarray(x, dtype=np.float32)
        m["stripD"] = t["stripD"].astype(ml_dtypes.bfloat16)
        for g in range(3):
            m["stripC%d" % g] = t["stripC%d" % g].astype(ml_dtypes.bfloat16)
        m["stripB"] = t["stripB"].astype(ml_dtypes.bfloat16)
        m["na_ic"] = t["na_ic"]
        m["na_mt"] = t["na_mt"]
        m["na_kr"] = t["na_kr"]
        in_maps.append(m)
    res = run_bass_kernel_spmd(nc, in_maps, core_ids=list(range(len(in_maps))))
    if DEBUG["on"]:
        DEBUG["res"] = res.results
    return [r["y"] for r in res.results]


def kernel(x_prompt, x_sample, **weights):
    x_prompt = np.asarray(x_prompt, dtype=np.float32)
    x_sample = np.asarray(x_sample, dtype=np.float32)
    D = x_prompt.shape[-1]
    DFF = np.asarray(weights["w_ffn1_out"]).shape[1]
    L = np.asarray(weights["w_out"]).shape[0]
    nb = x_prompt.shape[0]
    ns = x_sample.shape[0]
    assert nb % 2 == 0 and x_prompt.shape[1] == 2048 and x_sample.shape[1] == 4096
    cfg = Cfg(D=D, DFF=DFF, DEPTH=L, NTOK=2048)
    cfg.ncores = nb + 2 * ns
    xs = []
    kinds = []
    for i in range(nb):
        xs.append(x_prompt[i])
        kinds.append((True, i % 2))
    for i in range(ns):
        for r in range(2):
            xs.append(x_sample[i, r * 2048:(r + 1) * 2048])
            kinds.append((False, r))
    ys = _run(cfg, xs, kinds, weights)
    y_prompt = np.stack(ys[:nb], 0)
    y_sample = np.stack([np.concatenate([ys[nb + 2 * i], ys[nb + 2 * i + 1]], 0) for i in range(ns)], 0)
    return (y_prompt.astype(np.float32), y_sample.astype(np.float32))
```
